# Optimizing a Trainium2 kernel written in Bass

```python
import math
import jax, jax.numpy as jnp
from jax import lax
import numpy as np

D_MODEL = 4096
BATCH = 4
SEQ = 2048
DEPTH = 1

D_MIX = D_MODEL
ATT_WIDTH = D_MIX // 2
SSM_WIDTH = D_MIX - ATT_WIDTH

ATT_HEAD_DIM = 128
N_HEADS = ATT_WIDTH // ATT_HEAD_DIM
N_KV_HEADS = 4
KV_GROUP = N_HEADS // N_KV_HEADS
MOBA_BLOCK = 256
MOBA_TOPK = 3
Q_CHUNK = 16
ROPE_THETA = 10000.0

SSM_HEAD_DIM = 64
SSM_HEADS = SSM_WIDTH // SSM_HEAD_DIM
SSM_GROUPS = 4
SSM_STATE = 128
SSM_CHUNK = 256
CONV_WIDTH = 4
CONV_DIM = SSM_WIDTH + 2 * SSM_GROUPS * SSM_STATE

D_FF = -(-8 * D_MODEL // (3 * 256)) * 256

EPS = 1e-6

Q_DIM = N_HEADS * ATT_HEAD_DIM
KV_DIM = N_KV_HEADS * ATT_HEAD_DIM
IN_COLS = Q_DIM + 2 * KV_DIM + SSM_WIDTH + CONV_DIM + SSM_HEADS
IN_SPLITS = [Q_DIM, Q_DIM + KV_DIM, Q_DIM + 2 * KV_DIM,
             Q_DIM + 2 * KV_DIM + SSM_WIDTH,
             Q_DIM + 2 * KV_DIM + SSM_WIDTH + CONV_DIM]

kernel_name = 'hybrid_moba_mamba2_block'


def _round_up(n, m):
    return -(-n // m) * m


def _pad_seq(t, s_pad):
    pad = s_pad - t.shape[1]
    if pad == 0:
        return t
    widths = [(0, 0)] * t.ndim
    widths[1] = (0, pad)
    return jnp.pad(t, widths)


def rms_norm(x, w):
    x32 = x.astype(jnp.float32)
    y = x32 * lax.rsqrt(jnp.mean(x32 * x32, axis=-1, keepdims=True) + EPS)
    return (y * w.astype(jnp.float32)).astype(x.dtype)


def rope(t, positions):
    half = t.shape[-1] // 2
    inv_freq = ROPE_THETA ** (-jnp.arange(half, dtype=jnp.float32) / half)
    ang = positions.astype(jnp.float32)[..., None] * inv_freq
    cos = jnp.cos(ang)[:, :, None, :]
    sin = jnp.sin(ang)[:, :, None, :]
    t32 = t.astype(jnp.float32)
    t1, t2 = t32[..., :half], t32[..., half:]
    return jnp.concatenate([t1 * cos - t2 * sin, t2 * cos + t1 * sin], axis=-1).astype(t.dtype)


def moba_attention(q, k, v):
    b, s, _, dh = q.shape
    s_pad = _round_up(s, MOBA_BLOCK)
    q, k, v = _pad_seq(q, s_pad), _pad_seq(k, s_pad), _pad_seq(v, s_pad)
    nb = s_pad // MOBA_BLOCK
    n_sel = min(MOBA_TOPK, nb - 1)
    scale = dh ** -0.5
    qg = q.reshape(b, s_pad, N_KV_HEADS, KV_GROUP, dh).transpose(0, 2, 3, 1, 4)
    kb = k.transpose(0, 2, 1, 3).reshape(b, N_KV_HEADS, nb, MOBA_BLOCK, dh)
    vb = v.transpose(0, 2, 1, 3).reshape(b, N_KV_HEADS, nb, MOBA_BLOCK, dh)
    k_mean = jnp.mean(kb.astype(jnp.float32), axis=3)
    blk_ids = jnp.arange(nb)
    b_idx = jnp.arange(b)[:, None, None, None, None]
    h_idx = jnp.arange(N_KV_HEADS)[None, :, None, None, None]

    def chunk(start):
        qc = lax.dynamic_slice_in_dim(qg, start, Q_CHUNK, axis=3) * scale
        own = start // MOBA_BLOCK
        qpos = start + jnp.arange(Q_CHUNK)
        k_own = lax.dynamic_index_in_dim(kb, own, axis=2, keepdims=False)
        v_own = lax.dynamic_index_in_dim(vb, own, axis=2, keepdims=False)
        kpos = own * MOBA_BLOCK + jnp.arange(MOBA_BLOCK)
        s_own = jnp.einsum('bkgcd,bktd->bkgct', qc, k_own).astype(jnp.float32)
        s_own = jnp.where(kpos[None, :] <= qpos[:, None], s_own, -jnp.inf)
        if n_sel == 0:
            p_own = jax.nn.softmax(s_own, axis=-1).astype(v.dtype)
            return jnp.einsum('bkgct,bktd->bkgcd', p_own, v_own)
        gate = jnp.einsum('bkgcd,bknd->bkgcn', qc.astype(jnp.float32), k_mean)
        gate = jnp.where(blk_ids < own, gate, -jnp.inf)
        _, sel = lax.top_k(gate, n_sel)
        valid = sel < own
        k_sel = kb[b_idx, h_idx, sel]
        v_sel = vb[b_idx, h_idx, sel]
        s_past = jnp.einsum('bkgcd,bkgcjtd->bkgcjt', qc, k_sel).astype(jnp.float32)
        s_past = jnp.where(valid[..., None], s_past, -jnp.inf)
        s_past = s_past.reshape(s_past.shape[:4] + (n_sel * MOBA_BLOCK,))
        p = jax.nn.softmax(jnp.concatenate([s_past, s_own], axis=-1), axis=-1).astype(v.dtype)
        p_past = p[..., :n_sel * MOBA_BLOCK].reshape(p.shape[:4] + (n_sel, MOBA_BLOCK))
        p_own = p[..., n_sel * MOBA_BLOCK:]
        return (jnp.einsum('bkgcjt,bkgcjtd->bkgcd', p_past, v_sel)
                + jnp.einsum('bkgct,bktd->bkgcd', p_own, v_own))

    starts = jnp.arange(0, s_pad, Q_CHUNK)
    outs = lax.map(chunk, starts)
    out = outs.transpose(1, 0, 4, 2, 3, 5).reshape(b, s_pad, N_HEADS * dh)
    return out[:, :s]


def causal_depthwise_conv(u, w, bias):
    out = lax.conv_general_dilated(
        u, w[:, None, :].astype(u.dtype), window_strides=(1,),
        padding=[(CONV_WIDTH - 1, 0)], dimension_numbers=('NWC', 'WIO', 'NWC'),
        feature_group_count=u.shape[-1])
    return out + bias.astype(u.dtype)


def ssd_chunked(xs, dt, a, bm, cm):
    b, s = xs.shape[:2]
    s_pad = _round_up(s, SSM_CHUNK)
    nc = s_pad // SSM_CHUNK
    L = SSM_CHUNK
    g = SSM_GROUPS
    hg = SSM_HEADS // g
    f32 = jnp.float32
    xs = _pad_seq(xs.astype(f32), s_pad)
    dt = _pad_seq(dt, s_pad)
    bm = _pad_seq(bm.astype(f32), s_pad)
    cm = _pad_seq(cm.astype(f32), s_pad)
    x_dt = (xs * dt[..., None]).reshape(b, nc, L, g, hg, SSM_HEAD_DIM)
    a_dt = (dt * a).reshape(b, nc, L, g, hg).transpose(0, 3, 4, 1, 2)
    a_cum = jnp.cumsum(a_dt, axis=-1)
    bc = bm.reshape(b, nc, L, g, SSM_STATE)
    cc = cm.reshape(b, nc, L, g, SSM_STATE)
    causal = jnp.tril(jnp.ones((L, L), dtype=bool))
    seg = a_cum[..., :, None] - a_cum[..., None, :]
    decay = jnp.exp(jnp.where(causal, seg, -jnp.inf))
    cb = jnp.einsum('bclgn,bcsgn->bgcls', cc, bc)
    y_diag = jnp.einsum('bgcls,bghcls,bcsghp->bclghp', cb, decay, x_dt)
    state_decay = jnp.exp(a_cum[..., -1:] - a_cum)
    states = jnp.einsum('bcsgn,bghcs,bcsghp->bcghpn', bc, state_decay, x_dt)
    chunk_decay = jnp.exp(a_cum[..., -1])

    def step(h, inp):
        st, dec = inp
        return dec[..., None, None] * h + st, h

    h0 = jnp.zeros((b, g, hg, SSM_HEAD_DIM, SSM_STATE), f32)
    _, prev = lax.scan(step, h0, (jnp.moveaxis(states, 1, 0), jnp.moveaxis(chunk_decay, 3, 0)))
    prev = jnp.moveaxis(prev, 0, 1)
    y_off = jnp.einsum('bclgn,bcghpn,bghcl->bclghp', cc, prev, jnp.exp(a_cum))
    y = (y_diag + y_off).reshape(b, s_pad, SSM_HEADS, SSM_HEAD_DIM)
    return y[:, :s]


def mamba2_mixer(z, xbc, dt_raw, conv_w, conv_b, dt_bias, a_log, d_skip, norm_w):
    b, s, _ = z.shape
    xbc = jax.nn.silu(causal_depthwise_conv(xbc, conv_w, conv_b))
    xs, bm, cm = jnp.split(xbc, [SSM_WIDTH, SSM_WIDTH + SSM_GROUPS * SSM_STATE], axis=-1)
    xs = xs.reshape(b, s, SSM_HEADS, SSM_HEAD_DIM)
    bm = bm.reshape(b, s, SSM_GROUPS, SSM_STATE)
    cm = cm.reshape(b, s, SSM_GROUPS, SSM_STATE)
    dt = jax.nn.softplus(dt_raw.astype(jnp.float32) + dt_bias.astype(jnp.float32))
    a = -jnp.exp(a_log.astype(jnp.float32))
    y = ssd_chunked(xs, dt, a, bm, cm)
    y = y + xs.astype(jnp.float32) * d_skip.astype(jnp.float32)[:, None]
    y = y.reshape(b, s, SSM_WIDTH) * jax.nn.silu(z.astype(jnp.float32))
    yg = y.reshape(b, s, SSM_GROUPS, SSM_WIDTH // SSM_GROUPS)
    yg = yg * lax.rsqrt(jnp.mean(yg * yg, axis=-1, keepdims=True) + EPS)
    return (yg.reshape(b, s, SSM_WIDTH) * norm_w.astype(jnp.float32)).astype(z.dtype)


def setup_inputs(seed: int = 0) -> dict:
    key = jax.random.key(seed)
    ks = jax.random.split(key, 20)
    f32 = jnp.float32

    def nrm(k, shape, fan_in):
        return jax.random.normal(k, shape, f32) * fan_in ** -0.5

    def gain(k, n):
        return 1.0 + 0.05 * jax.random.normal(k, (DEPTH, n), f32)

    x = jax.random.normal(ks[0], (BATCH, SEQ, D_MODEL), f32)
    positions = jnp.broadcast_to(jnp.arange(SEQ, dtype=jnp.int32), (BATCH, SEQ))
    dt0 = jnp.exp(jax.random.uniform(ks[6], (DEPTH, SSM_HEADS), f32, math.log(1e-3), math.log(1e-1)))
    return {
        'x': x,
        'positions': positions,
        'mix_pre_norm': gain(ks[1], D_MODEL),
        'w_in': nrm(ks[2], (DEPTH, D_MODEL, IN_COLS), D_MODEL),
        'conv_w': nrm(ks[3], (DEPTH, CONV_WIDTH, CONV_DIM), CONV_WIDTH),
        'conv_b': 0.02 * jax.random.normal(ks[4], (DEPTH, CONV_DIM), f32),
        'dt_bias': dt0 + jnp.log(-jnp.expm1(-dt0)),
        'a_log': jnp.log(jax.random.uniform(ks[7], (DEPTH, SSM_HEADS), f32, 1.0, 16.0)),
        'd_skip': 1.0 + 0.1 * jax.random.normal(ks[8], (DEPTH, SSM_HEADS), f32),
        'ssm_norm': gain(ks[9], SSM_WIDTH),
        'w_out': nrm(ks[10], (DEPTH, D_MIX, D_MODEL), D_MIX),
        'mix_post_norm': gain(ks[11], D_MODEL),
        'ffn_pre_norm': gain(ks[12], D_MODEL),
        'w_gate': nrm(ks[13], (DEPTH, D_MODEL, D_FF), D_MODEL),
        'w_up': nrm(ks[14], (DEPTH, D_MODEL, D_FF), D_MODEL),
        'w_down': nrm(ks[15], (DEPTH, D_FF, D_MODEL), D_FF),
        'ffn_post_norm': gain(ks[16], D_MODEL),
    }


def reference(x, positions, mix_pre_norm, w_in, conv_w, conv_b, dt_bias, a_log, d_skip,
              ssm_norm, w_out, mix_post_norm, ffn_pre_norm, w_gate, w_up, w_down,
              ffn_post_norm):
    b, s, _ = x.shape
    for layer in range(DEPTH):
        h = rms_norm(x, mix_pre_norm[layer])
        proj = h @ w_in[layer]
        q, k, v, z, xbc, dt_raw = jnp.split(proj, IN_SPLITS, axis=-1)
        q = rope(q.reshape(b, s, N_HEADS, ATT_HEAD_DIM), positions)
        k = rope(k.reshape(b, s, N_KV_HEADS, ATT_HEAD_DIM), positions)
        v = v.reshape(b, s, N_KV_HEADS, ATT_HEAD_DIM)
        att = moba_attention(q, k, v)
        ssm = mamba2_mixer(z, xbc, dt_raw, conv_w[layer], conv_b[layer], dt_bias[layer],
                           a_log[layer], d_skip[layer], ssm_norm[layer])
        mixed = jnp.concatenate([att, ssm], axis=-1) @ w_out[layer]
        x = x + rms_norm(mixed, mix_post_norm[layer])
        h = rms_norm(x, ffn_pre_norm[layer])
        f = (jax.nn.silu(h @ w_gate[layer]) * (h @ w_up[layer])) @ w_down[layer]
        x = x + rms_norm(f, ffn_post_norm[layer])
    return x
```

```python
import numpy as np
import ml_dtypes
import concourse.bass as bass
import concourse.mybir as mybir
from concourse.bass_utils import run_bass_kernel_spmd

F32 = mybir.dt.float32
BF16 = mybir.dt.bfloat16
I32 = mybir.dt.int32
ALU = mybir.AluOpType
AF = mybir.ActivationFunctionType
AX = mybir.AxisListType

FULL_CFG = dict(D=4096, NKV=4, G=4, DFF=11008, B=4)
EPS = 1e-6
NEG = -30000.0
T = 1024
TC = 1024
NT = 8


def dsz(dt):
    return 4 if dt in (F32, I32) else 2


class Buf:
    __slots__ = ("t", "w", "r", "chan", "ccnt", "name")

    def __init__(self, t, name):
        self.t = t
        self.w = None
        self.r = {}
        self.chan = None
        self.ccnt = 0
        self.name = name

    def __getitem__(self, k):
        return self.t[k]


class Eng:
    def __init__(self, e, sem, name):
        self.e = e
        self.sem = sem
        self.cnt = 0
        self.seen = {}
        self.name = name
        self.pending = False


class KB:
    def __init__(self, nc):
        self.nc = nc
        self.sems = {}
        self.E = {}
        for n, e in (("pe", nc.tensor), ("act", nc.scalar), ("dve", nc.vector),
                     ("pool", nc.gpsimd), ("sp", nc.sync)):
            self.E[n] = Eng(e, self._sem("e_" + n), n)
        self.lo = 0
        self.hi = nc.sbuf_bytes_remaining
        self.base = None
        self.nid = 0
        self.peak = 0
        self.inflight = {}
        self.live = []
        self.max_inflight = 6

    def _sem(self, name):
        cm = self.nc.semaphore(name)
        s = cm.__enter__()
        self.sems[name] = (cm, s)
        return s

    def sb(self, name, shape, dt, side="L"):
        nbytes = int(np.prod(shape[1:])) * dsz(dt)
        nbytes = (nbytes + 63) // 64 * 64
        self.nid += 1
        if side == "L":
            off = self.lo
            self.lo += nbytes
        else:
            self.hi -= nbytes
            off = self.hi
        if self.lo > self.hi:
            raise RuntimeError(f"SBUF overflow allocating {name}: lo={self.lo} hi={self.hi}")
        self.peak = max(self.peak, self.lo + (self.nc_total - self.hi))
        t = self.nc.alloc_sbuf_tensor_at(f"{name}_{self.nid}", list(shape), dt, offset=off + self.off0)
        b = Buf(t, name)
        self.live.append(b)
        return b

    def mark(self):
        return (self.lo, self.hi, len(self.live))

    def release(self, m):
        self.lo, self.hi, n = m
        bufs = self.live[n:]
        del self.live[n:]
        for en in ("pe", "act", "dve", "pool", "sp"):
            self.wait_all(en, bufs)

    def _need(self, eng, sp):
        if sp is None:
            return
        sem, val = sp
        if eng.seen.get(id(sem), 0) < val:
            eng.e.wait_ge(sem, val)
            eng.seen[id(sem)] = val

    def _deps(self, eng, r, w, acc):
        for b in r:
            self._need(eng, b.w)
        for b in w:
            if not (acc and b.w is not None and b.w[0] is eng.sem):
                self._need(eng, b.w)
            for sem_id, sp in b.r.items():
                self._need(eng, sp)

    def _record(self, sp, r, w):
        for b in r:
            old = b.r.get(id(sp[0]))
            if old is None or old[1] < sp[1]:
                b.r[id(sp[0])] = sp
        for b in w:
            b.w = sp
            b.r = {}

    def op(self, en, fn, r=(), w=(), inc=True, acc=False):
        eng = self.E[en]
        self._deps(eng, r, w, acc)
        ins = fn(eng.e)
        if inc:
            eng.cnt += 1
            ins.then_inc(eng.sem, 1)
            eng.pending = False
            sp = (eng.sem, eng.cnt)
        else:
            eng.pending = True
            sp = (eng.sem, eng.cnt + 1)
        self._record(sp, r, w)
        return ins

    def dma(self, qn, out, in_, r=(), w=(), chan=None):
        eng = self.E[qn]
        self._deps(eng, r, w, False)
        q = self.inflight.setdefault(qn, [])
        if len(q) >= self.max_inflight:
            self._need(eng, q.pop(0))
        if chan.chan is None:
            chan.chan = self._sem(f"d{len(self.sems)}")
        ins = eng.e.dma_start(out=out, in_=in_)
        chan.ccnt += 16
        ins.then_inc(chan.chan, 16)
        sp = (chan.chan, chan.ccnt)
        q.append(sp)
        self._record(sp, r, w)
        return ins

    def flush(self):
        for n in ("pe",):
            eng = self.E[n]
            if eng.pending:
                raise RuntimeError("pending non-inc instruction at flush on " + n)

    def wait_all(self, en, bufs):
        eng = self.E[en]
        for b in bufs:
            self._need(eng, b.w)
            for sp in b.r.values():
                self._need(eng, sp)


def build(cfg, debug=None, stop=None):
    D = cfg["D"]; NKV = cfg["NKV"]; G = cfg["G"]; DFF = cfg["DFF"]
    KC = D // 128
    QW = NKV * 512; KW = NKV * 128; SW = G * 512; NH = G * 8
    q0 = 0; k0 = QW; v0 = QW + KW; z0 = QW + 2 * KW; x0 = z0 + SW
    b0 = x0 + SW; c0 = b0 + G * 128; dt0 = c0 + G * 128; INC = dt0 + NH
    DMIX = QW + SW
    MC = DMIX // 128
    AC = QW // 128
    NCT = (SW + 2 * G * 128) // 128
    FC = DFF // 128
    SCALE = 128 ** -0.5

    nc = bass.Bass("TRN2", target_bir_lowering=False)
    dram = {}

    def din(name, shape, dt=F32):
        dram[name] = nc.dram_tensor(name, list(shape), dt, kind="ExternalInput").ap()
        return dram[name]

    x_own = din("x_own", [T, D]); x_ctx = din("x_ctx", [TC, D])
    pos = din("pos", [128, TC + T], I32)
    flag = din("flag", [128, 1])
    gbias_d = din("gbias", [128, 4 * 8])
    w_in = din("w_in", [D, INC]); w_out = din("w_out", [DMIX, D])
    w_gate = din("w_gate", [D, DFF]); w_up = din("w_up", [D, DFF]); w_down = din("w_down", [DFF, D])
    wpre1_d = din("wpre1", [128, KC]); wpre2_d = din("wpre2", [128, KC])
    wpost1_d = din("wpost1", [128, D]); wpost2_d = din("wpost2", [128, D])
    wpre1b_d = din("wpre1b", [128, D]); wpre2b_d = din("wpre2b", [128, D])
    convw_d = din("convw", [128, NCT * 4]); convb_d = din("convb", [128, NCT])
    dtb_d = din("dtb", [128, NH]); alog_d = din("alog", [128, NH])
    dcol_d = din("dcol", [128, G * 4]); ncol_d = din("ncol", [128, G * 4])
    identb_d = din("identb", [128, 128], BF16); identf_d = din("identf", [128, 128])
    pmT_d = din("pmT", [128, 128]); tri_d = din("tri", [128, 2 * 256])
    maskT_d = din("maskT", [128, 2 * 256], BF16); trib_d = din("trib", [128, 128], BF16)
    sel8_d = din("sel8", [128, 8 * 128]); oh9_d = din("oh9", [128, 9 * 128], BF16)
    invf_d = din("invf", [128, 1])
    out_d = nc.dram_tensor("out", [T, D], F32, kind="ExternalOutput").ap()
    x1_d = nc.dram_tensor("x1_scratch", [T, D], F32, kind="Internal").ap()
    dbg = {}
    if debug:
        for name, shape, dt in debug:
            dbg[name] = nc.dram_tensor("dbg_" + name, list(shape), dt, kind="ExternalOutput").ap()

    K = KB(nc)
    K.off0 = 0
    total = nc.SBUF_PARTITION_SIZE_BYTES
    K.lo = (total - nc.sbuf_bytes_remaining + 63) // 64 * 64
    K.hi = total // 64 * 64 - 64
    K.nc_total = K.hi

    pbt = [nc.alloc_psum_tensor(f"pb{i}", [128, 512], F32) for i in range(8)]
    pb = [Buf(t, f"pb{i}") for i, t in enumerate(pbt)]

    def pf(i):
        return pb[i].t

    def pbf(i):
        return pb[i].t.bitcast(BF16)

    cst = {}

    def cload(name, d_ap, shape, dt=F32, side="R"):
        b = K.sb(name, shape, dt, side)
        K.dma("sp", b.t[:], d_ap, r=(), w=(b,), chan=b)
        cst[name] = b
        return b

    identb = cload("identb", identb_d, [128, 128], BF16)
    identf = cload("identf", identf_d, [128, 128])
    pmT = cload("pmT", pmT_d, [128, 128])
    tri = cload("tri", tri_d.rearrange("p (j l) -> p j l", j=2), [128, 2, 256])
    maskT = cload("maskT", maskT_d.rearrange("p (j l) -> p j l", j=2), [128, 2, 256], BF16)
    trib = cload("trib", trib_d, [128, 128], BF16)
    sel8 = cload("sel8", sel8_d.rearrange("p (h s) -> p h s", h=8), [128, 8, 128])
    oh9 = cload("oh9", oh9_d.rearrange("p (h s) -> p h s", h=9), [128, 9, 128], BF16)
    invf = cload("invf", invf_d, [128, 1])
    flg = cload("flag", flag, [128, 1])
    gbias = cload("gbias", gbias_d.rearrange("p (i j) -> p i j", i=4), [128, 4, 8])
    wpre1 = cload("wpre1", wpre1_d, [128, KC]); wpre2 = cload("wpre2", wpre2_d, [128, KC])
    convw = cload("convw", convw_d.rearrange("p (c k) -> p c k", k=4), [128, NCT, 4])
    convb = cload("convb", convb_d, [128, NCT])
    dtb = cload("dtb", dtb_d, [128, NH]); alog = cload("alog", alog_d, [128, NH])
    dcol = cload("dcol", dcol_d, [128, G * 4]); ncol = cload("ncol", ncol_d, [128, G * 4])
    onesf = K.sb("onesf", [128, 128], F32, "R")
    K.op("dve", lambda e: e.memset(onesf.t[:], 1.0), w=(onesf,))
    onesb = K.sb("onesb", [128, 128], BF16, "R")
    K.op("dve", lambda e: e.memset(onesb.t[:], 1.0), w=(onesb,))
    aneg = K.sb("aneg", [128, NH], F32, "R")
    K.op("act", lambda e: e.activation(out=aneg.t[:], in_=alog.t[:], func=AF.Exp), r=(alog,), w=(aneg,))
    K.op("dve", lambda e: e.tensor_scalar(out=aneg.t[:], in0=aneg.t[:], scalar1=-1.0, scalar2=None, op0=ALU.mult),
         r=(aneg,), w=(aneg,))
    epsc = K.sb("epsc", [128, 1], F32, "R")
    K.op("dve", lambda e: e.memset(epsc.t[:], EPS), w=(epsc,))
    negpi = K.sb("negpi", [128, 1], F32, "R")
    K.op("dve", lambda e: e.memset(negpi.t[:], -float(np.pi)), w=(negpi,))
    onec = K.sb("onec", [128, 1], F32, "R")
    K.op("dve", lambda e: e.memset(onec.t[:], 1.0), w=(onec,))

    if stop == "const":
        K.dma("sp", dbg["ST"][:, 0:128], onesf.t[:], r=(onesf, aneg, onesb, epsc, negpi, onec), w=(), chan=onesf)
        K.wait_all("sp", list(cst.values()) + [onesf]); return nc, dict(peak=K.peak)
    rr = {"ev": 0}

    def ev_eng():
        rr["ev"] ^= 1
        return "act" if rr["ev"] else "dve"

    def copy(en, out, in_, r, w):
        if en == "act":
            K.op("act", lambda e: e.copy(out=out, in_=in_), r=r, w=w)
        else:
            K.op(en, lambda e: e.tensor_copy(out, in_), r=r, w=w)

    def dump(name, buf, ap=None):
        if name in dbg:
            K.dma("sp", dbg[name], buf.t[:] if ap is None else ap, r=(buf,), w=(), chan=buf)

    def rope_tables(col0, n, tag):
        m = K.mark()
        pi_ = K.sb("posi", [128, n], I32)
        K.dma("sp", pi_.t[:], pos[:, col0:col0 + n], w=(pi_,), chan=pi_)
        ang = K.sb("ang", [128, n], F32)
        K.op("dve", lambda e: e.tensor_copy(ang.t[:], pi_.t[:]), r=(pi_,), w=(ang,))
        K.op("dve", lambda e: e.tensor_scalar(out=ang.t[:], in0=ang.t[:], scalar1=invf.t[:, 0:1], scalar2=None,
                                              op0=ALU.mult), r=(ang, invf), w=(ang,))
        tmp = K.sb("angt", [128, n], F32)
        K.release(m)
        cosT = K.sb("cos" + tag, [128, n], F32)
        sinT = K.sb("sin" + tag, [128, n], F32)
        return cosT, sinT

    def make_rope(col0, n, tag):
        cosT = K.sb("cos" + tag, [128, n], F32)
        sinT = K.sb("sin" + tag, [128, n], F32)
        m = K.mark()
        pi_ = K.sb("posi", [128, n], I32)
        K.dma("sp", pi_.t[:], pos[:, col0:col0 + n], w=(pi_,), chan=pi_)
        ang = K.sb("ang", [128, n], F32)
        tmp = K.sb("angt", [128, n], F32)
        K.op("dve", lambda e: e.tensor_copy(ang.t[:], pi_.t[:]), r=(pi_,), w=(ang,))
        K.op("dve", lambda e: e.tensor_scalar(out=ang.t[:], in0=ang.t[:], scalar1=invf.t[:, 0:1], scalar2=None,
                                              op0=ALU.mult), r=(ang, invf), w=(ang,))
        ki = K.sb("angi", [128, n], I32)
        inv2pi = float(1.0 / (2 * np.pi))
        for (dst, sh) in ((sinT, 0.5), (cosT, 0.75)):
            K.op("dve", lambda e: e.tensor_scalar(out=tmp.t[:], in0=ang.t[:], scalar1=inv2pi, scalar2=sh,
                                                  op0=ALU.mult, op1=ALU.add), r=(ang,), w=(tmp,))
            K.op("dve", lambda e: e.tensor_copy(ki.t[:], tmp.t[:]), r=(tmp,), w=(ki,))
            K.op("dve", lambda e: e.tensor_copy(dst.t[:], ki.t[:]), r=(ki,), w=(dst,))
            K.op("dve", lambda e: e.tensor_tensor(out=tmp.t[:], in0=tmp.t[:], in1=dst.t[:], op=ALU.subtract),
                 r=(tmp, dst), w=(tmp,))
            K.op("dve", lambda e: e.tensor_scalar(out=dst.t[:], in0=tmp.t[:], scalar1=0.0, scalar2=None,
                                                  op0=ALU.is_lt), r=(tmp,), w=(dst,))
            K.op("dve", lambda e: e.tensor_tensor(out=tmp.t[:], in0=tmp.t[:], in1=dst.t[:], op=ALU.add),
                 r=(tmp, dst), w=(tmp,))
            K.op("act", lambda e: e.activation(out=dst.t[:], in_=tmp.t[:], func=AF.Sin, bias=negpi.t[:, 0:1],
                                               scale=float(2 * np.pi)), r=(tmp, negpi), w=(dst,))
        K.wait_all("dve", [ki])
        K.wait_all("dve", [tmp, ang, pi_])
        K.wait_all("act", [tmp])
        K.wait_all("sp", [pi_])
        K.release(m)
        return cosT, sinT

    hbank = {"n": 0}

    def build_hT(x_d, ntok, hT, wb_d, rbuf=()):
        m = K.mark()
        wb = K.sb("wbc", [128, D], F32)
        K.dma("sp", wb.t[:], wb_d, w=(wb,), chan=wb)
        xt = [K.sb(f"xt{i}", [128, D], F32) for i in range(2)]
        hb = [K.sb(f"hb{i}", [128, D], BF16) for i in range(2)]
        st = [K.sb(f"st{i}", [128, 2], F32) for i in range(2)]
        ntile = ntok // 128

        def stageA(tt):
            x_, h_, s_ = xt[tt % 2], hb[tt % 2], st[tt % 2]
            K.dma("sp", x_.t[:], x_d[tt * 128:(tt + 1) * 128, :], r=rbuf, w=(x_,), chan=x_)
            K.op("act", lambda e: e.activation(out=h_.t[:], in_=x_.t[:], func=AF.Square, accum_out=s_.t[:, 0:1]),
                 r=(x_,), w=(h_, s_))
            K.op("act", lambda e: e.activation(out=s_.t[:, 1:2], in_=s_.t[:, 0:1], func=AF.Sqrt, bias=epsc.t[:, 0:1],
                                               scale=1.0 / D), r=(s_, epsc), w=(s_,))
            K.op("dve", lambda e: e.reciprocal(out=s_.t[:, 1:2], in_=s_.t[:, 1:2]), r=(s_,), w=(s_,))
            K.op("dve", lambda e: e.scalar_tensor_tensor(out=h_.t[:], in0=x_.t[:], scalar=s_.t[:, 1:2], in1=wb.t[:],
                                                         op0=ALU.mult, op1=ALU.mult), r=(x_, s_, wb), w=(h_,))

        def stageB(tt):
            h_ = hb[tt % 2]
            for k8 in range(0, KC, 8):
                bank = hbank["n"] % 8
                hbank["n"] += 1
                nk = min(8, KC - k8)
                for j in range(nk):
                    kc = k8 + j
                    K.op("pe", lambda e: e.transpose(out=pbf(bank)[:, j * 128:(j + 1) * 128],
                                                     in_=h_.t[:, kc * 128:(kc + 1) * 128], identity=identb.t[:]),
                         r=(h_, identb), w=(pb[bank],), inc=(j == nk - 1), acc=True)
                copy(ev_eng(), hT.t[:, k8:k8 + nk, tt * 128:(tt + 1) * 128],
                     pbf(bank)[:, 0:nk * 128].rearrange("p (k t) -> p k t", k=nk), r=(pb[bank],), w=(hT,))
        stageA(0)
        for tt in range(ntile):
            if tt + 1 < ntile:
                stageA(tt + 1)
            stageB(tt)
        K.release(m)

    class Ring:
        def __init__(self, n, width, name):
            self.bufs = [K.sb(f"{name}{i}", [128, 8, width], BF16) for i in range(n)]
            self.i = 0
            self.width = width

        def load(self, w_d, r0, nk, cc0, ncols):
            b = self.bufs[self.i % len(self.bufs)]
            self.i += 1
            K.dma("pool", b.t[:, 0:nk, 0:ncols],
                  w_d[r0:r0 + nk * 128, cc0:cc0 + ncols].rearrange("(k p) n -> p k n", p=128), w=(b,), chan=b)
            return b

        def drain(self):
            for e_ in ("pool", "pe"):
                K.wait_all(e_, self.bufs)

    def gemm_B(ring, w_d, cc0, ncols, xT, tok0, ntok, banks, cb, nK=None, mcols=None):
        nK = KC if nK is None else nK
        nct = (ncols + 127) // 128
        nh = ntok // 512
        slabs = []
        for s0 in range(0, nK, 8):
            nk = min(8, nK - s0)
            slabs.append((ring.load(w_d, s0 * 128, nk, cc0, ncols), s0, nk))
        for ct in range(nct):
            mc = min(128, ncols - ct * 128)
            bs = banks[ct * nh:(ct + 1) * nh]
            for (sl, s0, nk) in slabs:
                for j in range(nk):
                    kc = s0 + j
                    for h in range(nh):
                        last = (kc == nK - 1)
                        K.op("pe", lambda e: e.matmul(pf(bs[h])[0:mc, :], lhsT=sl.t[:, j, ct * 128:ct * 128 + mc],
                                                      rhs=xT.t[:, kc, tok0 + h * 512:tok0 + (h + 1) * 512],
                                                      start=(kc == 0), stop=last),
                             r=(sl, xT), w=(pb[bs[h]],), inc=(last or (j == nk - 1 and h == nh - 1)), acc=True)
            cb(ct, bs)

    def conv_silu(bs, ci, tail_in, tail_out, outs, scratch):
        xraw, acc = scratch
        if tail_in is None:
            K.op("dve", lambda e: e.memset(xraw.t[:, 0:3], 0.0), w=(xraw,))
        else:
            tb, tap = tail_in
            K.op("dve", lambda e: e.tensor_copy(xraw.t[:, 0:3], tap), r=(tb,), w=(xraw,))
        for h in range(2):
            copy("act", xraw.t[:, 3 + h * 512:3 + (h + 1) * 512], pf(bs[h])[:, :], r=(pb[bs[h]],), w=(xraw,))
        if tail_out is not None:
            tb, tap = tail_out
            K.op("dve", lambda e: e.tensor_copy(tap, xraw.t[:, 1024:1027]), r=(xraw,), w=(tb,))
        K.op("dve", lambda e: e.tensor_scalar(out=acc.t[:], in0=xraw.t[:, 3:1027], scalar1=convw.t[:, ci, 3:4],
                                              scalar2=convb.t[:, ci:ci + 1], op0=ALU.mult, op1=ALU.add),
             r=(xraw, convw, convb), w=(acc,))
        for k in range(3):
            K.op("dve", lambda e: e.scalar_tensor_tensor(out=acc.t[:], in0=xraw.t[:, k:k + 1024],
                                                         scalar=convw.t[:, ci, k:k + 1], in1=acc.t[:],
                                                         op0=ALU.mult, op1=ALU.add),
                 r=(xraw, convw, acc), w=(acc,))
        for (ap, b) in outs:
            K.op("act", lambda e: e.activation(out=ap, in_=acc.t[:], func=AF.Silu), r=(acc,), w=(b,))

    def rope_head(bs, cosT, sinT, raw, ro, rb=(6, 7)):
        for h in range(2):
            copy("act", raw.t[:, h * 512:(h + 1) * 512], pf(bs[h])[:, :], r=(pb[bs[h]],), w=(raw,))
        for h in range(2):
            sl = slice(h * 512, (h + 1) * 512)
            K.op("pe", lambda e: e.matmul(pf(rb[h])[:, :], lhsT=pmT.t[:], rhs=raw.t[:, sl], start=True, stop=True),
                 r=(pmT, raw), w=(pb[rb[h]],))
            K.op("dve", lambda e: e.tensor_tensor(out=ro.t[:, sl], in0=pf(rb[h])[:, :], in1=sinT.t[:, sl], op=ALU.mult),
                 r=(pb[rb[h]], sinT), w=(ro,))
        K.op("dve", lambda e: e.tensor_tensor(out=raw.t[:], in0=raw.t[:], in1=cosT.t[:], op=ALU.mult),
             r=(raw, cosT), w=(raw,))
        K.op("dve", lambda e: e.tensor_tensor(out=ro.t[:], in0=ro.t[:], in1=raw.t[:], op=ALU.add),
             r=(ro, raw), w=(ro,))

    def transpose_bf(src, src_ap_fn, ntiles, dst, dst_ap_fn, bank):
        for i0 in range(0, ntiles, 8):
            n = min(8, ntiles - i0)
            for j in range(n):
                K.op("pe", lambda e: e.transpose(out=pbf(bank)[:, j * 128:(j + 1) * 128], in_=src_ap_fn(i0 + j),
                                                 identity=identb.t[:]),
                     r=(src, identb), w=(pb[bank],), inc=(j == n - 1), acc=True)
            for j in range(n):
                copy(ev_eng(), dst_ap_fn(i0 + j), pbf(bank)[:, j * 128:(j + 1) * 128], r=(pb[bank],), w=(dst,))

    def dt_prep(ring, hT, P):
        m = K.mark()
        dtr = K.sb("dtr", [128, 1024], F32)

        def cb(ct, bs):
            for h in range(2):
                copy(ev_eng(), dtr.t[0:NH, h * 512:(h + 1) * 512], pf(bs[h])[0:NH, :], r=(pb[bs[h]],), w=(dtr,))
        gemm_B(ring, w_in, dt0, NH, hT, 0, 1024, [0, 1], cb)
        tmp = K.sb("dtt", [128, NT, NH], F32)
        tmp2 = K.sb("dtt2", [128, NT, NH], F32)
        adt = K.sb("adt", [128, NT, NH], F32)
        acum = K.sb("acum", [128, NT, NH], F32)
        tot = K.sb("tot", [128, 4, NH], F32)
        dt, dtsd, nacum, cd = P["dt"], P["dtsd"], P["nacum"], P["cd"]
        for tt in range(NT):
            K.op("pe", lambda e: e.transpose(out=pf(4)[:, tt * NH:(tt + 1) * NH], in_=dtr.t[0:NH, tt * 128:(tt + 1) * 128],
                                             identity=identf.t[0:NH, 0:NH]), r=(dtr, identf), w=(pb[4],),
                 inc=(tt == NT - 1), acc=True)
        p4 = pf(4)[:, 0:NT * NH].rearrange("p (t h) -> p t h", t=NT)
        K.op("dve", lambda e: e.tensor_tensor(out=tmp.t[:], in0=p4,
                                              in1=dtb.t[:].unsqueeze(1).broadcast_to([128, NT, NH]), op=ALU.add),
             r=(pb[4], dtb), w=(tmp,))
        K.op("act", lambda e: e.activation(out=tmp2.t[:], in_=tmp.t[:], func=AF.Abs), r=(tmp,), w=(tmp2,))
        K.op("act", lambda e: e.activation(out=tmp2.t[:], in_=tmp2.t[:], func=AF.Exp, scale=-1.0), r=(tmp2,), w=(tmp2,))
        K.op("act", lambda e: e.activation(out=tmp2.t[:], in_=tmp2.t[:], func=AF.Ln, bias=onec.t[:, 0:1], scale=1.0),
             r=(tmp2, onec), w=(tmp2,))
        K.op("dve", lambda e: e.scalar_tensor_tensor(out=dt.t[:], in0=tmp.t[:], scalar=0.0, in1=tmp2.t[:],
                                                     op0=ALU.max, op1=ALU.add), r=(tmp, tmp2), w=(dt,))
        K.op("dve", lambda e: e.tensor_tensor(out=adt.t[:], in0=dt.t[:],
                                              in1=aneg.t[:].unsqueeze(1).broadcast_to([128, NT, NH]), op=ALU.mult),
             r=(dt, aneg), w=(adt,))
        for c in range(4):
            for lt in range(2):
                for j in range(2):
                    K.op("pe", lambda e: e.matmul(pf(5)[:, (c * 2 + lt) * NH:(c * 2 + lt + 1) * NH],
                                                  lhsT=tri.t[:, j, lt * 128:(lt + 1) * 128], rhs=adt.t[:, 2 * c + j, :],
                                                  start=(j == 0), stop=(j == 1)),
                         r=(tri, adt), w=(pb[5],), inc=(c == 3 and lt == 1 and j == 1), acc=True)
            for j in range(2):
                K.op("pe", lambda e: e.matmul(pf(6)[:, c * NH:(c + 1) * NH], lhsT=onesf.t[:], rhs=adt.t[:, 2 * c + j, :],
                                              start=(j == 0), stop=(j == 1)),
                     r=(onesf, adt), w=(pb[6],), inc=(c == 3 and j == 1), acc=True)
        copy("dve", acum.t[:], pf(5)[:, 0:NT * NH].rearrange("p (t h) -> p t h", t=NT), r=(pb[5],), w=(acum,))
        copy("dve", tot.t[:], pf(6)[:, 0:4 * NH].rearrange("p (c h) -> p c h", c=4), r=(pb[6],), w=(tot,))
        K.op("act", lambda e: e.activation(out=cd.t[:], in_=tot.t[:], func=AF.Exp), r=(tot,), w=(cd,))
        K.op("dve", lambda e: e.tensor_scalar(out=nacum.t[:], in0=acum.t[:], scalar1=-1.0, scalar2=None, op0=ALU.mult),
             r=(acum,), w=(nacum,))
        for c in range(4):
            K.op("dve", lambda e: e.tensor_tensor(out=tmp.t[:, 2 * c:2 * c + 2, :], in0=nacum.t[:, 2 * c:2 * c + 2, :],
                                                  in1=tot.t[:, c:c + 1, :].broadcast_to([128, 2, NH]), op=ALU.add),
                 r=(nacum, tot), w=(tmp,))
        K.op("act", lambda e: e.activation(out=tmp.t[:], in_=tmp.t[:], func=AF.Exp), r=(tmp,), w=(tmp,))
        K.op("dve", lambda e: e.tensor_tensor(out=dtsd.t[:], in0=tmp.t[:], in1=dt.t[:], op=ALU.mult),
             r=(tmp, dt), w=(dtsd,))
        for e_ in ("act", "dve", "pe"):
            K.wait_all(e_, [dtr, tmp, tmp2, adt, acum, tot])
        K.release(m)

    def make_acumT(P, g):
        dst = P["acumT"][g]
        for tt in range(NT):
            K.op("pe", lambda e: e.transpose(out=pf(7)[0:8, (tt % 4) * 128:(tt % 4 + 1) * 128],
                                             in_=P["nacum"].t[:, tt, g * 8:(g + 1) * 8], identity=identf.t[:]),
                 r=(P["nacum"], identf), w=(pb[7],), inc=(tt % 4 == 3), acc=True)
            if tt % 4 == 3:
                K.op("act", lambda e: e.activation(out=dst.t[0:8, (tt // 4) * 512:(tt // 4 + 1) * 512], in_=pf(7)[0:8, 0:512],
                                                   func=AF.Copy, scale=-1.0), r=(pb[7],), w=(dst,))

    mL0 = K.mark()
    ST = K.sb("ST", [128, NH * 64], F32)
    K.op("dve", lambda e: e.memset(ST.t[:], 0.0), w=(ST,))
    tails = K.sb("tails", [128, NCT, 3], F32)
    kT = [K.sb(f"kT{h}", [128, TC], BF16) for h in range(NKV)]
    vtok = [K.sb(f"vtok{h}", [128, NT, 132], BF16) for h in range(NKV)]
    for h in range(NKV):
        K.op("dve", lambda e: e.memset(vtok[h].t[:, :, 128:132], 1.0), w=(vtok[h],))
    kms = K.sb("kms", [128, NKV, 8], F32)
    kmax = K.sb("kmax", [128, NKV, 4], F32)
    mP1 = K.mark()

    def kv_phase(hT, blk_base, cosT, sinT, ring, scr, heads, kdst, vdst, rb=(6, 7)):
        raw, ro, vT = scr
        sq = raw
        for h in heads:
            def cbk(ct, bs, h=h):
                rope_head(bs, cosT, sinT, raw, ro, rb)
                copy("act", kdst[h].t[:, 0:1024], ro.t[:], r=(ro,), w=(kdst[h],))
                bi = blk_base
                K.op("dve", lambda e: e.tensor_reduce(out=kms.t[:, h, bi:bi + 4],
                                                      in_=ro.t[:].rearrange("p (b t) -> p b t", b=4),
                                                      axis=AX.X, op=ALU.add), r=(ro,), w=(kms,))
                K.op("act", lambda e: e.activation(out=sq.t[:], in_=ro.t[:], func=AF.Square), r=(ro,), w=(sq,))
                for hh in range(2):
                    K.op("pe", lambda e: e.matmul(pf(rb[hh])[:, :], lhsT=onesf.t[:], rhs=sq.t[:, hh * 512:(hh + 1) * 512],
                                                  start=True, stop=True), r=(onesf, sq), w=(pb[rb[hh]],))
                    pi = blk_base // 2 + hh
                    K.op("dve", lambda e: e.tensor_reduce(out=kmax.t[:, h, pi:pi + 1], in_=pf(rb[hh])[:, :],
                                                          axis=AX.X, op=ALU.max), r=(pb[rb[hh]],), w=(kmax,))
            gemm_B(ring, w_in, k0 + h * 128, 128, hT, 0, 1024, [0, 1], cbk)

            def cbv(ct, bs, h=h):
                for hh in range(2):
                    copy(ev_eng(), vT.t[:, hh * 512:(hh + 1) * 512], pf(bs[hh])[:, :], r=(pb[bs[hh]],), w=(vT,))
                transpose_bf(vT, lambda i: vT.t[:, i * 128:(i + 1) * 128], NT, vdst[h],
                             lambda i: vdst[h].t[:, i, 0:128], 5)
            gemm_B(ring, w_in, v0 + h * 128, 128, hT, 0, 1024, [2, 3], cbv)

    def ssd_state_update(c, ti, Btok, xdtsd, cd, bank=6):
        h0 = ti * 2
        for j in range(2):
            K.op("pe", lambda e: e.matmul(pf(bank)[:, 256:384], lhsT=Btok.t[:, 2 * c + j, :], rhs=xdtsd.t[:, 2 * c + j, :],
                                          start=(j == 0), stop=(j == 1)), r=(Btok, xdtsd), w=(pb[bank],), inc=(j == 1), acc=True)
        sl = ST.t[:, ti * 128:(ti + 1) * 128].rearrange("p (h d) -> p h d", h=2)
        K.op("dve", lambda e: e.tensor_tensor(out=sl, in0=sl,
                                              in1=cd.t[:, c, h0:h0 + 2].unsqueeze(2).broadcast_to([128, 2, 64]),
                                              op=ALU.mult), r=(ST, cd), w=(ST,))
        K.op("dve", lambda e: e.tensor_tensor(out=ST.t[:, ti * 128:(ti + 1) * 128], in0=ST.t[:, ti * 128:(ti + 1) * 128],
                                              in1=pf(bank)[:, 256:384], op=ALU.add), r=(ST, pb[bank]), w=(ST,))

    def xs_tile_prep(ring, hT, ti, tail_in, tail_out, P, scr, xsf, xdt, xdtsd):
        xsb, xtok = scr["xsb"], scr["xtok"]
        h0 = ti * 2

        def cb(ct, bs):
            outs = [(xsb.t[:], xsb)]
            if xsf is not None:
                outs.append((xsf.t[:], xsf))
            conv_silu(bs, ti, tail_in, tail_out, outs, (scr["xraw"], scr["acc"]))
        gemm_B(ring, w_in, x0 + ti * 128, 128, hT, 0, 1024, [0, 1], cb)
        transpose_bf(xsb, lambda i: xsb.t[:, i * 128:(i + 1) * 128], NT, xtok, lambda i: xtok.t[:, i, :], 5)
        x4 = xtok.t[:].rearrange("p t (h d) -> p t h d", h=2)
        if xdt is not None:
            K.op("dve", lambda e: e.tensor_tensor(out=xdt.t[:].rearrange("p t (h d) -> p t h d", h=2), in0=x4,
                                                  in1=P["dt"].t[:, :, h0:h0 + 2].unsqueeze(3).broadcast_to([128, NT, 2, 64]),
                                                  op=ALU.mult), r=(xtok, P["dt"]), w=(xdt,))
        K.op("dve", lambda e: e.tensor_tensor(out=xdtsd.t[:].rearrange("p t (h d) -> p t h d", h=2), in0=x4,
                                              in1=P["dtsd"].t[:, :, h0:h0 + 2].unsqueeze(3).broadcast_to([128, NT, 2, 64]),
                                              op=ALU.mult), r=(xtok, P["dtsd"]), w=(xdtsd,))

    def bc_tile(ring, hT, ci, tail_in, tail_out, outT, scr):
        def cb(ct, bs):
            conv_silu(bs, ci, tail_in, tail_out, [(outT.t[:], outT)] if outT is not None else [], (scr["xraw"], scr["acc"]))
        col = (b0 + (ci - 4 * G) * 128) if ci < 5 * G else (c0 + (ci - 5 * G) * 128)
        gemm_B(ring, w_in, col, 128, hT, 0, 1024, [2, 3], cb)

    hT_ctx = K.sb("hT_ctx", [128, KC, TC], BF16)
    build_hT(x_ctx, TC, hT_ctx, wpre1b_d)
    dump("hT_ctx", hT_ctx)
    if stop == "hT":
        K.wait_all("sp", [hT_ctx]); return nc, dict(peak=K.peak)
    ring = Ring(4, 128, "slab")
    cosC, sinC = make_rope(0, TC, "C")
    scr_kv = (K.sb("kraw", [128, 1024], F32), K.sb("kro", [128, 1024], F32), K.sb("vT", [128, 1024], BF16))
    kv_phase(hT_ctx, 0, cosC, sinC, ring, scr_kv, list(range(NKV)), kT, vtok)
    dump("kT0", kT[0], kT[0].t[:, 0:1024])
    if stop == "kv":
        K.wait_all("sp", [hT_ctx, kT[0]]); return nc, dict(peak=K.peak)
    P = {n: K.sb(n + "C", [128, NT, NH], F32) for n in ("dt", "dtsd", "nacum")}
    P["cd"] = K.sb("cdC", [128, 4, NH], F32)
    dt_prep(ring, hT_ctx, P)
    if stop == "dt":
        K.wait_all("sp", [hT_ctx, kT[0]]); return nc, dict(peak=K.peak)
    scrs = [{"xsb": K.sb(f"xsb{i}", [128, 1024], BF16), "xtok": K.sb(f"xtok{i}", [128, NT, 128], BF16),
             "xraw": K.sb(f"xraw{i}", [128, 1027], F32), "acc": K.sb(f"acc{i}", [128, 1024], F32)} for i in range(2)]
    xdtsds = [K.sb(f"xdtsd{i}", [128, NT, 128], BF16) for i in range(2)]
    BTs = [K.sb(f"BT{i}", [128, 1024], BF16) for i in range(2)]
    Btoks = [K.sb(f"Btok{i}", [128, NT, 128], BF16) for i in range(2)]
    tiles1 = [(g, hp) for g in range(G) for hp in range(4)]

    def emit_BC(g):
        sc = scrs[g % 2]
        BT_, Bk_ = BTs[g % 2], Btoks[g % 2]
        bc_tile(ring, hT_ctx, 4 * G + g, None, (tails, tails.t[:, 4 * G + g, :]), BT_, sc)
        transpose_bf(BT_, lambda i: BT_.t[:, i * 128:(i + 1) * 128], NT, Bk_, lambda i: Bk_.t[:, i, :], 5)
        bc_tile(ring, hT_ctx, 5 * G + g, None, (tails, tails.t[:, 5 * G + g, :]), None, sc)

    def gemm_xs1(t):
        g, hp = tiles1[t]
        ti = g * 4 + hp
        sc = scrs[t % 2]

        def cb(ct, bs):
            conv_silu(bs, ti, None, (tails, tails.t[:, ti, :]), [(sc["xsb"].t[:], sc["xsb"])], (sc["xraw"], sc["acc"]))
        gemm_B(ring, w_in, x0 + ti * 128, 128, hT_ctx, 0, 1024, [0, 1], cb)

    def post1(t):
        g, hp = tiles1[t]
        ti = g * 4 + hp
        h0 = ti * 2
        sc = scrs[t % 2]
        xd = xdtsds[t % 2]
        xsb, xtok = sc["xsb"], sc["xtok"]
        transpose_bf(xsb, lambda i: xsb.t[:, i * 128:(i + 1) * 128], NT, xtok, lambda i: xtok.t[:, i, :], 5)
        K.op("dve", lambda e: e.tensor_tensor(out=xd.t[:].rearrange("p t (h d) -> p t h d", h=2),
                                              in0=xtok.t[:].rearrange("p t (h d) -> p t h d", h=2),
                                              in1=P["dtsd"].t[:, :, h0:h0 + 2].unsqueeze(3).broadcast_to([128, NT, 2, 64]),
                                              op=ALU.mult), r=(xtok, P["dtsd"]), w=(xd,))
        for c in range(4):
            ssd_state_update(c, ti, Btoks[g % 2], xd, P["cd"], bank=6 + c % 2)

    emit_BC(0)
    gemm_xs1(0)
    for t in range(len(tiles1)):
        if t + 1 < len(tiles1):
            if tiles1[t + 1][1] == 0:
                emit_BC(tiles1[t + 1][0])
            gemm_xs1(t + 1)
        post1(t)
    K.op("dve", lambda e: e.tensor_scalar(out=ST.t[:], in0=ST.t[:], scalar1=flg.t[:, 0:1], scalar2=None, op0=ALU.mult),
         r=(ST, flg), w=(ST,))
    dump("ST", ST)
    ring.drain()
    for e_ in ("pe", "act", "dve", "pool", "sp"):
        K.wait_all(e_, [hT_ctx, cosC, sinC] + list(scr_kv) + list(P.values()))
    K.release(mP1)

    mR = K.mark()
    mixS = K.sb("mixS", [128, MC - AC, T], BF16, "R")
    hT_own = K.sb("hT_own", [128, KC, T], BF16)
    build_hT(x_own, T, hT_own, wpre1b_d)
    mP2 = K.mark()
    ring = Ring(4, 128, "slab")
    P = {n: K.sb(n + "O", [128, NT, NH], F32) for n in ("dt", "dtsd", "nacum")}
    P["cd"] = K.sb("cdO", [128, 4, NH], F32)
    acumT1 = K.sb("acumT", [128, 1024], F32)
    P["acumT"] = [acumT1 for g in range(G)]
    dt_prep(ring, hT_own, P)
    scr = {"xsb": K.sb("xsb", [128, 1024], BF16),
           "xraw": K.sb("xraw", [128, 1027], F32), "acc": K.sb("acc", [128, 1024], F32)}
    xdt = K.sb("xdt", [128, NT, 128], BF16)
    xdtsd = K.sb("xdtsd", [128, NT, 128], BF16)
    BT = K.sb("BT", [128, 1024], BF16); CT = K.sb("CT", [128, 1024], BF16)
    Btok = K.sb("Btok", [128, NT, 128], BF16)
    xsf = K.sb("xsf", [128, 1024], F32); zs = scr["xraw"]
    yg = K.sb("yg", [128, 4, 1024], F32)
    STbp = [K.sb(f"STbp{i}", [128, 128], BF16) for i in range(2)]
    cbms = [K.sb(f"cbm{i}", [128, 2, 256], F32) for i in range(2)]
    m01 = K.sb("m01", [128, 2, 256], BF16)
    K.op("dve", lambda e: e.tensor_scalar(out=m01.t[:], in0=maskT.t[:], scalar1=-1.0, scalar2=None, op0=ALU.is_ge),
         r=(maskT,), w=(m01,))
    decs = [K.sb(f"dec{i}", [128, 2, 256], F32) for i in range(2)]
    MTs = [[K.sb(f"MT{p}{i}", [128, 2, 256], BF16) for i in range(2)] for p in range(2)]
    eas = [K.sb(f"ea{i}", [128, 256], F32) for i in range(2)]
    Css = [[K.sb(f"Cs{p}{i}", [128, 256], BF16) for i in range(2)] for p in range(2)]
    rs = scr["acc"]

    class _Stop(Exception):
        pass

    chkc = {}

    def chk(name):
        chkc[name] = chkc.get(name, 0) + 1
        if stop == name or stop == f"{name}@{chkc[name]}":
            raise _Stop()
    try:
      for g in range(G):
          make_acumT(P, g)
          bc_tile(ring, hT_own, 4 * G + g, (tails, tails.t[:, 4 * G + g, :]), None, BT, scr)
          transpose_bf(BT, lambda i: BT.t[:, i * 128:(i + 1) * 128], NT, Btok, lambda i: Btok.t[:, i, :], 5)
          bc_tile(ring, hT_own, 5 * G + g, (tails, tails.t[:, 5 * G + g, :]), None, CT, scr)
          for hp in range(4):
              ti = g * 4 + hp
              h0 = ti * 2

              def cbx(ct, bs, ti=ti):
                  conv_silu(bs, ti, (tails, tails.t[:, ti, :]), None, [(scr["xsb"].t[:], scr["xsb"]), (xsf.t[:], xsf)],
                            (scr["xraw"], scr["acc"]))
              gemm_B(ring, w_in, x0 + ti * 128, 128, hT_own, 0, 1024, [0, 1], cbx)

              def cbz(ct, bs):
                  for hh in range(2):
                      K.op("act", lambda e: e.activation(out=zs.t[:, hh * 512:(hh + 1) * 512], in_=pf(bs[hh])[:, :],
                                                         func=AF.Silu), r=(pb[bs[hh]],), w=(zs,))
              gemm_B(ring, w_in, z0 + ti * 128, 128, hT_own, 0, 1024, [0, 1], cbz)
              xsb = scr["xsb"]
              for i8 in range(NT):
                  K.op("pe", lambda e: e.transpose(out=pbf(5)[:, i8 * 128:(i8 + 1) * 128], in_=xsb.t[:, i8 * 128:(i8 + 1) * 128],
                                                   identity=identb.t[:]), r=(xsb, identb), w=(pb[5],), inc=(i8 == NT - 1), acc=True)
              x4 = pbf(5)[:, :].rearrange("p (t h d) -> p t h d", t=NT, h=2)
              K.op("dve", lambda e: e.tensor_tensor(out=xdt.t[:].rearrange("p t (h d) -> p t h d", h=2), in0=x4,
                                                    in1=P["dt"].t[:, :, h0:h0 + 2].unsqueeze(3).broadcast_to([128, NT, 2, 64]),
                                                    op=ALU.mult), r=(pb[5], P["dt"]), w=(xdt,))
              K.op("dve", lambda e: e.tensor_tensor(out=xdtsd.t[:].rearrange("p t (h d) -> p t h d", h=2), in0=x4,
                                                    in1=P["dtsd"].t[:, :, h0:h0 + 2].unsqueeze(3).broadcast_to([128, NT, 2, 64]),
                                                    op=ALU.mult), r=(pb[5], P["dtsd"]), w=(xdtsd,))
              copy("act", STbp[0].t[:], ST.t[:, ti * 128:(ti + 1) * 128], r=(ST,), w=(STbp[0],))

              def stage1(c, hp=hp, ti=ti, g=g):
                  cs = slice(c * 256, (c + 1) * 256)
                  cb_ = cbms[c % 2]
                  for j in range(2):
                      K.op("pe", lambda e: e.matmul(pf(4)[:, j * 256:(j + 1) * 256], lhsT=BT.t[:, (2 * c + j) * 128:(2 * c + j + 1) * 128],
                                                    rhs=CT.t[:, cs], start=True, stop=True), r=(BT, CT), w=(pb[4],), inc=(j == 1), acc=True)
                  copy("act", cb_.t[:], pf(4)[:, :].rearrange("p (j l) -> p j l", j=2), r=(pb[4],), w=(cb_,))
                  K.op("dve", lambda e: e.tensor_tensor(out=cb_.t[:], in0=cb_.t[:], in1=m01.t[:], op=ALU.mult), r=(cb_, m01), w=(cb_,))
                  for e2 in range(2):
                      lh = hp * 2 + e2
                      K.op("pe", lambda e: e.matmul(pf(5)[:, e2 * 256:(e2 + 1) * 256], lhsT=sel8.t[0:8, lh, :],
                                                    rhs=P["acumT"][g].t[0:8, cs], start=True, stop=True),
                           r=(sel8, P["acumT"][g]), w=(pb[5],), inc=(e2 == 1), acc=True)
                  for e2 in range(2):
                      Rv = pf(5)[:, e2 * 256:(e2 + 1) * 256]
                      K.op("act", lambda e: e.activation(out=eas[e2].t[:], in_=Rv, func=AF.Exp), r=(pb[5],), w=(eas[e2],))
                  for e2 in range(2):
                      h = ti * 2 + e2
                      Rv = pf(5)[:, e2 * 256:(e2 + 1) * 256]
                      MT_, Cs_ = MTs[c % 2][e2], Css[c % 2][e2]
                      for j in range(2):
                          K.op("dve", lambda e: e.tensor_scalar(out=decs[e2].t[:, j, :], in0=Rv, scalar1=P["nacum"].t[:, 2 * c + j, h:h + 1],
                                                                scalar2=0.0, op0=ALU.add, op1=ALU.min),
                               r=(pb[5], P["nacum"], eas[0], eas[1]), w=(decs[e2],))
                      K.op("act", lambda e: e.activation(out=decs[e2].t[:], in_=decs[e2].t[:], func=AF.Exp), r=(decs[e2],), w=(decs[e2],))
                      K.op("dve", lambda e: e.tensor_tensor(out=Cs_.t[:], in0=CT.t[:, cs], in1=eas[e2].t[:], op=ALU.mult),
                           r=(CT, eas[e2]), w=(Cs_,))

              def stage1b(c):
                  cb_ = cbms[c % 2]
                  for e2 in range(2):
                      MT_ = MTs[c % 2][e2]
                      K.op("dve", lambda e: e.tensor_tensor(out=MT_.t[:], in0=cb_.t[:], in1=decs[e2].t[:], op=ALU.mult),
                           r=(cb_, decs[e2]), w=(MT_,))

              def stage2(c, hp=hp, ti=ti):
                  cs = slice(c * 256, (c + 1) * 256)
                  for e2 in range(2):
                      MT_, Cs_ = MTs[c % 2][e2], Css[c % 2][e2]
                      yb = pf(6 + e2)[:, 0:256]
                      for j in range(2):
                          K.op("pe", lambda e: e.matmul(yb, lhsT=xdt.t[:, 2 * c + j, :], rhs=MT_.t[:, j, :], start=(j == 0), stop=False),
                               r=(xdt, MT_), w=(pb[6 + e2],), inc=False, acc=True)
                      K.op("pe", lambda e: e.matmul(yb, lhsT=STbp[c % 2].t[:], rhs=Cs_.t[:], start=False, stop=True),
                           r=(STbp[c % 2], Cs_), w=(pb[6 + e2],), acc=True)
                      rows = slice(e2 * 64, (e2 + 1) * 64)
                      K.op("dve", lambda e: e.scalar_tensor_tensor(out=yg.t[rows, hp, cs], in0=xsf.t[rows, cs],
                                                                   scalar=dcol.t[rows, ti:ti + 1], in1=pf(6 + e2)[rows, 0:256],
                                                                   op0=ALU.mult, op1=ALU.add), r=(xsf, dcol, pb[6 + e2]), w=(yg,))

              def state(c, ti=ti):
                  ssd_state_update(c, ti, Btok, xdtsd, P["cd"], bank=0)
                  copy("act", STbp[(c + 1) % 2].t[:], ST.t[:, ti * 128:(ti + 1) * 128], r=(ST,), w=(STbp[(c + 1) % 2],))

              stage1(0)
              stage1b(0)
              for c in range(4):
                  if c + 1 < 4:
                      stage1(c + 1)
                  stage2(c)
                  state(c)
                  if c + 1 < 4:
                      stage1b(c + 1)
              chk("s6")
              K.op("dve", lambda e: e.tensor_tensor(out=yg.t[:, hp, :], in0=yg.t[:, hp, :], in1=zs.t[:, 0:1024], op=ALU.mult),
                   r=(yg, zs), w=(yg,))
              K.op("act", lambda e: e.activation(out=scr["acc"].t[:], in_=yg.t[:, hp, :], func=AF.Square), r=(yg,), w=(scr["acc"],))
              for hh in range(2):
                  K.op("pe", lambda e: e.matmul(pf(2 + hh)[:, :], lhsT=onesf.t[:], rhs=scr["acc"].t[:, hh * 512:(hh + 1) * 512],
                                                start=(hp == 0), stop=(hp == 3)), r=(onesf, scr["acc"]), w=(pb[2 + hh],), acc=True)
          for hh in range(2):
              K.op("act", lambda e: e.activation(out=rs.t[:, hh * 512:(hh + 1) * 512], in_=pf(2 + hh)[:, :], func=AF.Sqrt,
                                                 bias=epsc.t[:, 0:1], scale=1.0 / 512), r=(pb[2 + hh], epsc), w=(rs,))
          K.op("dve", lambda e: e.reciprocal(out=rs.t[:], in_=rs.t[:]), r=(rs,), w=(rs,))
          for hp in range(4):
              ti = g * 4 + hp
              K.op("dve", lambda e: e.scalar_tensor_tensor(out=mixS.t[:, ti, :], in0=yg.t[:, hp, :], scalar=ncol.t[:, ti:ti + 1],
                                                           in1=rs.t[:], op0=ALU.mult, op1=ALU.mult), r=(yg, ncol, rs), w=(mixS,))
    except _Stop:
        K.wait_all("sp", [ST, kT[0], hT_ctx])
        for en in ("pe", "act", "dve", "pool", "sp"):
            K.wait_all(en, K.live)
        return nc, dict(peak=K.peak)
    dump("ssmT", mixS)
    if stop == "ssm":
        K.wait_all("sp", [mixS, ST, kT[0], hT_ctx]); return nc, dict(peak=K.peak)
    ring.drain()
    K.release(mP2)

    mixA = K.sb("mixA", [128, AC, T], BF16, "R")
    mP2b = K.mark()
    ring = Ring(4, 128, "slab")
    cosO, sinO = make_rope(TC, T, "O")
    qTs = [K.sb(f"qT{i}", [128, 1024], BF16) for i in range(2)]
    scr_kv = (K.sb("kraw", [128, 1024], F32), K.sb("kro", [128, 1024], F32), qTs[1])
    raw, ro, _ = scr_kv
    qsq = raw
    kTo1 = K.sb("kTo", [128, T], BF16); vtoko1 = K.sb("vtoko", [128, NT, 132], BF16)
    K.op("dve", lambda e: e.memset(vtoko1.t[:, :, 128:132], 1.0), w=(vtoko1,))
    kTo = [kTo1] * NKV; vtoko = [vtoko1] * NKV
    kmeanT = K.sb("kmeanT", [128, 8], BF16)
    kmx = K.sb("kmx", [128, 1], F32)
    gb8 = K.sb("gb8", [128, 8, 8], F32)
    for qt in range(8):
        K.op("dve", lambda e: e.tensor_copy(gb8.t[:, qt, :], gbias.t[:, qt // 2, :]), r=(gbias,), w=(gb8,))
    gm = K.sb("gm", [128, 8, 8], F32); g2 = K.sb("g2", [128, 8, 8], F32); msk = K.sb("msk", [128, 8, 8], F32)
    m1 = K.sb("m1", [128, 8], F32)
    sels = [K.sb(f"sel{i}", [128, 8, 8], F32) for i in range(2)]
    cbs_ = [K.sb(f"cb{i}", [128, 1], F32) for i in range(2)]
    qmx = K.sb("qmx", [128, 2], F32)
    NPT = 3
    pT = [K.sb(f"pT{i}", [128, 256], BF16) for i in range(NPT)]
    Oacc = K.sb("Oacc", [128, 8, 132], F32)
    onrm = K.sb("onrm", [128, 128], BF16)
    rdc = K.sb("rdc", [128, 1], F32)
    RB = (0, 1)

    def ksl(kv, kt):
        return (kT[kv], kT[kv].t[:, kt * 128:(kt + 1) * 128]) if kt < 8 else (kTo[kv], kTo[kv].t[:, (kt - 8) * 128:(kt - 7) * 128])

    def vsl(kv, kt):
        return (vtok[kv], vtok[kv].t[:, kt, 0:129]) if kt < 8 else (vtoko[kv], vtoko[kv].t[:, kt - 8, 0:129])

    def bmax(dst, src):
        K.op("dve", lambda e: e.tensor_reduce(out=dst.t[:], in_=src.t[:], axis=AX.X, op=ALU.max), r=(src,), w=(dst,))

    def knock(dst, src, m):
        K.op("dve", lambda e: e.tensor_tensor(out=msk.t[:], in0=src.t[:], in1=m.t[:].unsqueeze(2).broadcast_to([128, 8, 8]),
                                              op=ALU.is_ge), r=(src, m), w=(msk,))
        K.op("dve", lambda e: e.scalar_tensor_tensor(out=dst.t[:], in0=msk.t[:], scalar=-3e30, in1=src.t[:],
                                                     op0=ALU.mult, op1=ALU.add), r=(msk, src), w=(dst,))

    def prepA(kv, g, qT_, cb_):
        def cbq(ct, bs):
            rope_head(bs, cosO, sinO, raw, ro, RB)
            copy("act", qT_.t[:], ro.t[:], r=(ro,), w=(qT_,))
            K.op("act", lambda e: e.activation(out=qsq.t[:], in_=ro.t[:], func=AF.Square), r=(ro,), w=(qsq,))
            for hh in range(2):
                K.op("pe", lambda e: e.matmul(pf(RB[hh])[:, :], lhsT=onesf.t[:], rhs=qsq.t[:, hh * 512:(hh + 1) * 512],
                                              start=True, stop=True), r=(onesf, qsq), w=(pb[RB[hh]],))
                K.op("dve", lambda e: e.tensor_reduce(out=qmx.t[:, hh:hh + 1], in_=pf(RB[hh])[:, :], axis=AX.X, op=ALU.max),
                     r=(pb[RB[hh]],), w=(qmx,))
            K.op("dve", lambda e: e.tensor_tensor(out=qmx.t[:, 0:1], in0=qmx.t[:, 0:1], in1=qmx.t[:, 1:2], op=ALU.max),
                 r=(qmx,), w=(qmx,))
            K.op("act", lambda e: e.activation(out=cb_.t[:], in_=qmx.t[:, 0:1], func=AF.Sqrt, scale=kmx.t[:, 0:1]),
                 r=(qmx, kmx), w=(cb_,))
            K.op("dve", lambda e: e.tensor_scalar(out=cb_.t[:], in0=cb_.t[:], scalar1=-SCALE, scalar2=None, op0=ALU.mult),
                 r=(cb_,), w=(cb_,))
        gemm_B(ring, w_in, q0 + (kv * 4 + g) * 128, 128, hT_own, 0, 1024, [0, 1], cbq)

    def prepB(qT_, sel_):
        for qt in range(NT):
            ts_ = slice(qt * 128, (qt + 1) * 128)
            K.op("pe", lambda e: e.matmul(pf(RB[0])[:, qt * 8:qt * 8 + 8], lhsT=qT_.t[:, ts_], rhs=kmeanT.t[:, 0:8], start=True, stop=True),
                 r=(qT_, kmeanT), w=(pb[RB[0]],), inc=(qt == NT - 1), acc=True)
        Gv = pf(RB[0])[:, 0:64].rearrange("p (t c) -> p t c", c=8)
        K.op("dve", lambda e: e.tensor_tensor(out=gm.t[:], in0=Gv, in1=gb8.t[:], op=ALU.add), r=(pb[RB[0]], gb8), w=(gm,))
        bmax(m1, gm); knock(g2, gm, m1)
        bmax(m1, g2); knock(g2, g2, m1)
        bmax(m1, g2)
        K.op("dve", lambda e: e.tensor_scalar(out=m1.t[:], in0=m1.t[:], scalar1=-1e29, scalar2=None, op0=ALU.max), r=(m1,), w=(m1,))
        K.op("dve", lambda e: e.tensor_tensor(out=sel_.t[:], in0=gm.t[:], in1=m1.t[:].unsqueeze(2).broadcast_to([128, 8, 8]),
                                              op=ALU.is_ge), r=(gm, m1), w=(sel_,))

    state = {"n": 0, "blk": 0}

    def main_items(kv, g):
        items = []
        for i in range(4):
            nbk = 4 + i + 1
            for j in range(nbk):
                for u in range(2):
                    own = (j == nbk - 1)
                    items.append(dict(i=i, j=j, kt=j * 2 + u, u=u, own=own))
        return items

    def emit_qk(kv, qT_, cb_, it):
        n = state["n"]; state["n"] += 1
        it["p"] = pT[n % NPT]
        bank = 2 + (n % 2)
        qs = slice(it["i"] * 256, (it["i"] + 1) * 256)
        kb, kap = ksl(kv, it["kt"])
        K.op("pe", lambda e: e.matmul(pf(bank)[:, 0:256], lhsT=kap, rhs=qT_.t[:, qs], start=True, stop=(not it["own"])),
             r=(kb, qT_), w=(pb[bank],), inc=(not it["own"]), acc=True)
        if it["own"]:
            K.op("pe", lambda e: e.matmul(pf(bank)[:, 0:256], lhsT=identb.t[:], rhs=maskT.t[:, it["u"], :], start=False, stop=True),
                 r=(identb, maskT), w=(pb[bank],), acc=True)
        K.op("act", lambda e: e.activation(out=it["p"].t[:], in_=pf(bank)[:, 0:256], func=AF.Exp, scale=SCALE, bias=cb_.t[:, 0:1]),
             r=(pb[bank], cb_), w=(it["p"],))

    def emit_pv(kv, g, sel_, it):
        par = state["blk"] % 2
        vb, vap = vsl(kv, it["kt"])
        for w_ in range(2):
            bank = 4 + par * 2 + w_
            K.op("pe", lambda e: e.matmul(pf(bank)[:, 0:129], lhsT=it["p"].t[:, w_ * 128:(w_ + 1) * 128], rhs=vap,
                                          start=(it["u"] == 0), stop=(it["u"] == 1)),
                 r=(it["p"], vb), w=(pb[bank],), inc=(it["u"] == 1), acc=True)
        if it["u"] == 1:
            for w_ in range(2):
                bank = 4 + par * 2 + w_
                qt = it["i"] * 2 + w_
                if it["j"] == 0:
                    K.op("dve", lambda e: e.tensor_scalar(out=Oacc.t[:, qt, 0:129], in0=pf(bank)[:, 0:129],
                                                          scalar1=sel_.t[:, qt, 0:1], scalar2=None, op0=ALU.mult),
                         r=(pb[bank], sel_), w=(Oacc,))
                elif not it["own"]:
                    K.op("dve", lambda e: e.scalar_tensor_tensor(out=Oacc.t[:, qt, 0:129], in0=pf(bank)[:, 0:129],
                                                                 scalar=sel_.t[:, qt, it["j"]:it["j"] + 1], in1=Oacc.t[:, qt, 0:129],
                                                                 op0=ALU.mult, op1=ALU.add), r=(pb[bank], sel_, Oacc), w=(Oacc,))
                else:
                    K.op("dve", lambda e: e.tensor_tensor(out=Oacc.t[:, qt, 0:129], in0=pf(bank)[:, 0:129], in1=Oacc.t[:, qt, 0:129],
                                                          op=ALU.add), r=(pb[bank], Oacc), w=(Oacc,))
                    K.op("dve", lambda e: e.reciprocal(out=rdc.t[:], in_=Oacc.t[:, qt, 128:129]), r=(Oacc,), w=(rdc,))
                    K.op("dve", lambda e: e.tensor_scalar(out=onrm.t[:], in0=Oacc.t[:, qt, 0:128], scalar1=rdc.t[:, 0:1], scalar2=None,
                                                          op0=ALU.mult), r=(Oacc, rdc), w=(onrm,))
                    K.op("pe", lambda e: e.transpose(out=pbf(RB[1])[:, 0:128], in_=onrm.t[:], identity=identb.t[:]),
                         r=(onrm, identb), w=(pb[RB[1]],))
                    copy("act", mixA.t[:, kv * 4 + g, qt * 128:(qt + 1) * 128], pbf(RB[1])[:, 0:128], r=(pb[RB[1]],), w=(mixA,))
            state["blk"] += 1

    LA = 2
    for kv in range(NKV):
        kv_phase(hT_own, 4, cosO, sinO, ring, scr_kv, [kv], kTo, vtoko, RB)
        K.op("dve", lambda e: e.tensor_scalar(out=kmeanT.t[:], in0=kms.t[:, kv, :], scalar1=1.0 / 256, scalar2=None, op0=ALU.mult),
             r=(kms,), w=(kmeanT,))
        K.op("dve", lambda e: e.tensor_reduce(out=kmx.t[:], in_=kmax.t[:, kv, :], axis=AX.X, op=ALU.max), r=(kmax,), w=(kmx,))
        prepA(kv, 0, qTs[0], cbs_[0]); prepB(qTs[0], sels[0])
        for g in range(4):
            qT_, sel_, cb_ = qTs[g % 2], sels[g % 2], cbs_[g % 2]
            items = main_items(kv, g)
            ni = len(items)
            hooks = {}
            if g < 3:
                nq, nsl, ncb = qTs[(g + 1) % 2], sels[(g + 1) % 2], cbs_[(g + 1) % 2]
                hooks = {ni // 8: (lambda nq=nq, ncb=ncb, g=g: prepA(kv, g + 1, nq, ncb)),
                         (5 * ni) // 8: (lambda nq=nq, nsl=nsl: prepB(nq, nsl))}
            for n in range(ni + LA):
                if n in hooks:
                    hooks[n]()
                if n < ni:
                    emit_qk(kv, qT_, cb_, items[n])
                if n - LA >= 0:
                    emit_pv(kv, g, sel_, items[n - LA])
    dump("attT", mixA)
    if stop == "att":
        K.wait_all("sp", [mixA, mixS, ST, kT[0], hT_ctx]); return nc, dict(peak=K.peak)
    ring.drain()
    K.release(mP2b)

    K.release((mL0[0], K.hi, mL0[2]))
    x1buf = Buf(None, "x1_dram")
    mP3 = K.mark()
    ringA = Ring(3, 512, "slabA")
    wpost = K.sb("wpost", [128, D], F32)
    K.dma("sp", wpost.t[:], wpost1_d, w=(wpost,), chan=wpost)
    mixed = K.sb("mixed", [128, 4, D], F32)
    xt = [K.sb("xt3", [128, D], F32)]
    st3 = K.sb("st3", [128, 2], F32)

    def norm_res_store(src, tt, gt, x_src_d, x_rbuf, dst_d, dst_buf, wp, x_, wp_d=None):
        K.op("act", lambda e: e.activation(out=x_.t[:], in_=src.t[:, tt, :], func=AF.Square, accum_out=st3.t[:, 0:1]),
             r=(src,), w=(x_, st3))
        K.op("act", lambda e: e.activation(out=st3.t[:, 1:2], in_=st3.t[:, 0:1], func=AF.Sqrt, bias=epsc.t[:, 0:1], scale=1.0 / D),
             r=(st3, epsc), w=(st3,))
        K.op("dve", lambda e: e.reciprocal(out=st3.t[:, 1:2], in_=st3.t[:, 1:2]), r=(st3,), w=(st3,))
        K.dma("sp", x_.t[:], x_src_d[gt * 128:(gt + 1) * 128, :], r=x_rbuf, w=(x_,), chan=x_)
        if wp_d is None:
            K.op("dve", lambda e: e.scalar_tensor_tensor(out=src.t[:, tt, :], in0=src.t[:, tt, :], scalar=st3.t[:, 1:2], in1=wp.t[:],
                                                         op0=ALU.mult, op1=ALU.mult), r=(src, st3, wp), w=(src,))
        else:
            hd = D // 2
            for hf in range(2):
                K.dma("sp", wp.t[:], wp_d[:, hf * hd:(hf + 1) * hd], w=(wp,), chan=wp)
                K.op("dve", lambda e: e.scalar_tensor_tensor(out=src.t[:, tt, hf * hd:(hf + 1) * hd], in0=src.t[:, tt, hf * hd:(hf + 1) * hd],
                                                             scalar=st3.t[:, 1:2], in1=wp.t[:], op0=ALU.mult, op1=ALU.mult),
                     r=(src, st3, wp), w=(src,))
        K.op("dve", lambda e: e.tensor_tensor(out=x_.t[:], in0=x_.t[:], in1=src.t[:, tt, :], op=ALU.add), r=(x_, src), w=(x_,))
        K.dma("sp", dst_d[gt * 128:(gt + 1) * 128, :], x_.t[:], r=(x_,), w=(dst_buf,), chan=x_)

    def gemm_A(ring_, w_d, nK, actbuf, tok_tiles, dst, ncg):
        ntt = len(tok_tiles)
        for cg in range(ncg):
            base = (cg % 2) * 4
            for s0 in range(0, nK, 8):
                nk = min(8, nK - s0)
                sl = ring_.load(w_d, s0 * 128, nk, cg * 512, 512)
                for j in range(nk):
                    kc = s0 + j
                    for i, t0 in enumerate(tok_tiles):
                        last = (kc == nK - 1)
                        ab, akc = actbuf(kc) if callable(actbuf) else (actbuf, kc)
                        K.op("pe", lambda e: e.matmul(pf(base + i)[:, :], lhsT=ab.t[:, akc, t0:t0 + 128], rhs=sl.t[:, j, :],
                                                      start=(kc == 0), stop=last), r=(ab, sl), w=(pb[base + i],),
                             inc=(last or (j == nk - 1 and i == ntt - 1)), acc=True)
            for i in range(ntt):
                copy(ev_eng(), dst.t[:, i, cg * 512:(cg + 1) * 512], pf(base + i)[:, :], r=(pb[base + i],), w=(dst,))

    mixmap = lambda kc: (mixA, kc) if kc < AC else (mixS, kc - AC)
    for tg in range(2):
        gemm_A(ringA, w_out, MC, mixmap, [(tg * 4 + i) * 128 for i in range(4)], mixed, D // 512)
        for tt in range(4):
            gt = tg * 4 + tt
            norm_res_store(mixed, tt, gt, x_own, (), x1_d, x1buf, wpost, xt[0])
    ringA.drain()
    K.release(mP3)
    K.release((K.lo, mR[1], mR[2]))
    if stop == "mix":
        K.wait_all("sp", [x1buf]); return nc, dict(peak=K.peak)

    outbuf = Buf(None, "out_dram")
    K.release((mL0[0], K.nc_total, 0))
    identb = K.sb("identb2", [128, 128], BF16)
    K.dma("sp", identb.t[:], identb_d, w=(identb,), chan=identb)
    wpre2 = K.sb("wpre2b", [128, KC], F32)
    K.dma("sp", wpre2.t[:], wpre2_d, w=(wpre2,), chan=wpre2)
    epsc = K.sb("epsc2", [128, 1], F32)
    K.op("dve", lambda e: e.memset(epsc.t[:], EPS), w=(epsc,))
    for tg in range(2):
        mF = K.mark()
        actT = K.sb("actT", [128, FC, 512], BF16)
        mF2 = K.mark()
        h2T = K.sb("h2T", [128, KC, 512], BF16)
        build_hT(x1_d[tg * 512:(tg + 1) * 512, :], 512, h2T, wpre2b_d, rbuf=(x1buf,))
        ring = Ring(8, 256, "slabF")
        sg_ = K.sb("sgate", [128, 512], F32)
        for gi, cc0 in enumerate(range(0, DFF, 256)):
            ncols = min(256, DFF - cc0)
            nct = (ncols + 127) // 128
            bset = (gi % 2) * 4
            gemm_B(ring, w_gate, cc0, ncols, h2T, 0, 512, [bset, bset + 1][:nct], lambda ct, bs: None)

            def cbu(ct, bs, cc0=cc0, bset=bset):
                jt = cc0 // 128 + ct
                K.op("act", lambda e: e.activation(out=sg_.t[:], in_=pf(bset + ct)[:, :], func=AF.Silu), r=(pb[bset + ct],), w=(sg_,))
                K.op("dve", lambda e: e.tensor_tensor(out=actT.t[:, jt, :], in0=sg_.t[:], in1=pf(bs[0])[:, :], op=ALU.mult),
                     r=(sg_, pb[bs[0]]), w=(actT,))
            gemm_B(ring, w_up, cc0, ncols, h2T, 0, 512, [bset + 2, bset + 3][:nct], cbu)
        ring.drain()
        K.release(mF2)
        ringA = Ring(3, 512, "slabD")
        wph = K.sb("wpost2h", [128, D // 2], F32)
        f = K.sb("f", [128, 4, D], F32)
        xt = [K.sb("xt4", [128, D], F32)]
        st3 = K.sb("st4", [128, 2], F32)
        gemm_A(ringA, w_down, FC, actT, [i * 128 for i in range(4)], f, D // 512)
        for tt in range(4):
            gt = tg * 4 + tt
            norm_res_store(f, tt, gt, x1_d, (x1buf,), out_d, outbuf, wph, xt[0], wp_d=wpost2_d)
        ringA.drain()
        K.release(mF)
    K.wait_all("sp", [outbuf])
    K.flush()
    return nc, dict(peak=K.peak)

    outs = [b for b in [ST, kT[0], hT_ctx] if b.chan is not None]
    K.wait_all("sp", outs)
    K.flush()
    return nc, dict(peak=K.peak)


def host_consts(cfg):
    D = cfg["D"]; G = cfg["G"]
    c = {}
    c["identb"] = np.eye(128, dtype=np.float32).astype(ml_dtypes.bfloat16)
    c["identf"] = np.eye(128, dtype=np.float32)
    pm = np.zeros((128, 128), np.float32)
    for d in range(64):
        pm[d + 64, d] = -1.0
        pm[d, d + 64] = 1.0
    c["pmT"] = pm
    tri = np.zeros((128, 2, 256), np.float32)
    mT = np.zeros((128, 2, 256), np.float32)
    for j in range(2):
        s = j * 128 + np.arange(128)[:, None]
        l = np.arange(256)[None, :]
        tri[:, j, :] = (s <= l)
        mT[:, j, :] = np.where(l >= s, 0.0, NEG)
    c["tri"] = tri.reshape(128, 512)
    c["maskT"] = mT.reshape(128, 512).astype(ml_dtypes.bfloat16)
    t = np.arange(128)[:, None]; q = np.arange(128)[None, :]
    c["trib"] = np.where(t <= q, 0.0, NEG).astype(np.float32).astype(ml_dtypes.bfloat16)
    sel8 = np.zeros((128, 8, 128), np.float32)
    for h in range(8):
        sel8[h, h, :] = 1.0
    c["sel8"] = sel8.reshape(128, 1024)
    oh9 = np.zeros((128, 9, 128), np.float32)
    for h in range(9):
        oh9[h, h, :] = 1.0
    c["oh9"] = oh9.reshape(128, 9 * 128).astype(ml_dtypes.bfloat16)
    half = 64
    inv = (10000.0 ** (-np.arange(half, dtype=np.float32) / half)).astype(np.float32)
    c["invf"] = np.concatenate([inv, inv]).reshape(128, 1).astype(np.float32)
    return c


def host_inputs(cfg, inputs, core):
    D = cfg["D"]; G = cfg["G"]; NKV = cfg["NKV"]
    NH = G * 8
    b, half = core // 2, core % 2
    m = {}
    x = inputs["x"]
    m["x_own"] = np.ascontiguousarray(x[b, half * T:(half + 1) * T])
    m["x_ctx"] = np.ascontiguousarray(x[b, 0:TC]) if half == 1 else np.zeros((TC, D), np.float32)
    p = inputs["positions"][b].astype(np.int32)
    if half == 1:
        pp = np.concatenate([p[0:TC], p[TC:TC + T]])
    else:
        pp = np.concatenate([np.zeros(TC, np.int32), p[0:T]])
    m["pos"] = np.ascontiguousarray(np.broadcast_to(pp[None, :], (128, TC + T)))
    m["flag"] = np.full((128, 1), float(half), np.float32)
    gb = np.zeros((4, 8), np.float32)
    for i in range(4):
        for j in range(8):
            valid = (j < 4 + i) and (j >= 4 or half == 1)
            gb[i, j] = 0.0 if valid else -1e30
    m["gbias"] = np.ascontiguousarray(np.broadcast_to(gb.reshape(1, 32), (128, 32)))
    m["w_in"] = inputs["w_in"][0]; m["w_out"] = inputs["w_out"][0]
    m["w_gate"] = inputs["w_gate"][0]; m["w_up"] = inputs["w_up"][0]; m["w_down"] = inputs["w_down"][0]
    col = lambda v: np.ascontiguousarray(v.reshape(-1, 128).T)
    rep = lambda v: np.ascontiguousarray(np.broadcast_to(v.reshape(1, -1), (128, v.size)))
    m["wpre1"] = col(inputs["mix_pre_norm"][0]); m["wpre2"] = col(inputs["ffn_pre_norm"][0])
    m["wpost1"] = rep(inputs["mix_post_norm"][0]); m["wpost2"] = rep(inputs["ffn_post_norm"][0])
    m["wpre1b"] = rep(inputs["mix_pre_norm"][0]); m["wpre2b"] = rep(inputs["ffn_pre_norm"][0])
    cw = inputs["conv_w"][0]
    NCT = cw.shape[1] // 128
    m["convw"] = np.ascontiguousarray(cw.T.reshape(NCT, 128, 4).transpose(1, 0, 2).reshape(128, NCT * 4))
    m["convb"] = col(inputs["conv_b"][0])
    m["dtb"] = rep(inputs["dt_bias"][0]); m["alog"] = rep(inputs["a_log"][0])
    m["dcol"] = col(np.repeat(inputs["d_skip"][0], 64)); m["ncol"] = col(inputs["ssm_norm"][0])
    m.update(host_consts(cfg))
    return m


def kernel(**inputs):
    cfg = FULL_CFG
    inputs = {k: np.asarray(v) for k, v in inputs.items()}
    nc, _ = build(cfg)
    n = 2 * cfg["B"]
    in_maps = [host_inputs(cfg, inputs, c) for c in range(n)]
    res = run_bass_kernel_spmd(nc, in_maps, core_ids=list(range(n)))
    out = np.empty((cfg["B"], 2 * T, cfg["D"]), np.float32)
    for c in range(n):
        out[c // 2, (c % 2) * T:(c % 2 + 1) * T] = np.asarray(res.results[c]["out"], dtype=np.float32)
    return out
```

```python
import numpy as np
import ml_dtypes
import concourse.bass as bass
import concourse.mybir as mybir
from concourse.bass_utils import run_bass_kernel_spmd

F32 = mybir.dt.float32
BF16 = mybir.dt.bfloat16
I32 = mybir.dt.int32
ALU = mybir.AluOpType
AF = mybir.ActivationFunctionType
AX = mybir.AxisListType

FULL_CFG = dict(D=4096, NKV=4, G=4, DFF=11008, B=4)
EPS = 1e-6
NEG = -30000.0
T = 1024
TC = 1024
NT = 8


def dsz(dt):
    return 4 if dt in (F32, I32) else 2


class Buf:
    __slots__ = ("t", "w", "r", "chan", "ccnt", "name")

    def __init__(self, t, name):
        self.t = t
        self.w = None
        self.r = {}
        self.chan = None
        self.ccnt = 0
        self.name = name

    def __getitem__(self, k):
        return self.t[k]


class Eng:
    def __init__(self, e, sem, name):
        self.e = e
        self.sem = sem
        self.cnt = 0
        self.seen = {}
        self.name = name
        self.pending = False


class KB:
    def __init__(self, nc):
        self.nc = nc
        self.sems = {}
        self.E = {}
        for n, e in (("pe", nc.tensor), ("act", nc.scalar), ("dve", nc.vector),
                     ("pool", nc.gpsimd), ("sp", nc.sync)):
            self.E[n] = Eng(e, self._sem("e_" + n), n)
        self.lo = 0
        self.hi = nc.sbuf_bytes_remaining
        self.base = None
        self.nid = 0
        self.peak = 0
        self.inflight = {}
        self.live = []
        self.max_inflight = 6

    def _sem(self, name):
        cm = self.nc.semaphore(name)
        s = cm.__enter__()
        self.sems[name] = (cm, s)
        return s

    def sb(self, name, shape, dt, side="L"):
        nbytes = int(np.prod(shape[1:])) * dsz(dt)
        nbytes = (nbytes + 63) // 64 * 64
        self.nid += 1
        if side == "L":
            off = self.lo
            self.lo += nbytes
        else:
            self.hi -= nbytes
            off = self.hi
        if self.lo > self.hi:
            raise RuntimeError(f"SBUF overflow allocating {name}: lo={self.lo} hi={self.hi}")
        self.peak = max(self.peak, self.lo + (self.nc_total - self.hi))
        t = self.nc.alloc_sbuf_tensor_at(f"{name}_{self.nid}", list(shape), dt, offset=off + self.off0)
        b = Buf(t, name)
        self.live.append(b)
        return b

    def mark(self):
        return (self.lo, self.hi, len(self.live))

    def release(self, m):
        self.lo, self.hi, n = m
        bufs = self.live[n:]
        del self.live[n:]
        for en in ("pe", "act", "dve", "pool", "sp"):
            self.wait_all(en, bufs)

    def _need(self, eng, sp):
        if sp is None:
            return
        sem, val = sp
        if eng.seen.get(id(sem), 0) < val:
            eng.e.wait_ge(sem, val)
            eng.seen[id(sem)] = val

    def _deps(self, eng, r, w, acc):
        for b in r:
            self._need(eng, b.w)
        for b in w:
            if not (acc and b.w is not None and b.w[0] is eng.sem):
                self._need(eng, b.w)
            for sem_id, sp in b.r.items():
                self._need(eng, sp)

    def _record(self, sp, r, w):
        for b in r:
            old = b.r.get(id(sp[0]))
            if old is None or old[1] < sp[1]:
                b.r[id(sp[0])] = sp
        for b in w:
            b.w = sp
            b.r = {}

    def op(self, en, fn, r=(), w=(), inc=True, acc=False):
        eng = self.E[en]
        self._deps(eng, r, w, acc)
        ins = fn(eng.e)
        if inc:
            eng.cnt += 1
            ins.then_inc(eng.sem, 1)
            eng.pending = False
            sp = (eng.sem, eng.cnt)
        else:
            eng.pending = True
            sp = (eng.sem, eng.cnt + 1)
        self._record(sp, r, w)
        return ins

    def dma(self, qn, out, in_, r=(), w=(), chan=None):
        eng = self.E[qn]
        self._deps(eng, r, w, False)
        q = self.inflight.setdefault(qn, [])
        if len(q) >= self.max_inflight:
            self._need(eng, q.pop(0))
        if chan.chan is None:
            chan.chan = self._sem(f"d{len(self.sems)}")
        ins = eng.e.dma_start(out=out, in_=in_)
        chan.ccnt += 16
        ins.then_inc(chan.chan, 16)
        sp = (chan.chan, chan.ccnt)
        q.append(sp)
        self._record(sp, r, w)
        return ins

    def flush(self):
        for n in ("pe",):
            eng = self.E[n]
            if eng.pending:
                raise RuntimeError("pending non-inc instruction at flush on " + n)

    def wait_all(self, en, bufs):
        eng = self.E[en]
        for b in bufs:
            self._need(eng, b.w)
            for sp in b.r.values():
                self._need(eng, sp)


def build(cfg, debug=None, stop=None):
    D = cfg["D"]; NKV = cfg["NKV"]; G = cfg["G"]; DFF = cfg["DFF"]
    KC = D // 128
    QW = NKV * 512; KW = NKV * 128; SW = G * 512; NH = G * 8
    q0 = 0; k0 = QW; v0 = QW + KW; z0 = QW + 2 * KW; x0 = z0 + SW
    b0 = x0 + SW; c0 = b0 + G * 128; dt0 = c0 + G * 128; INC = dt0 + NH
    DMIX = QW + SW
    MC = DMIX // 128
    AC = QW // 128
    NCT = (SW + 2 * G * 128) // 128
    FC = DFF // 128
    SCALE = 128 ** -0.5

    nc = bass.Bass("TRN2", target_bir_lowering=False)
    dram = {}

    def din(name, shape, dt=F32):
        dram[name] = nc.dram_tensor(name, list(shape), dt, kind="ExternalInput").ap()
        return dram[name]

    x_own = din("x_own", [T, D]); x_ctx = din("x_ctx", [TC, D])
    pos = din("pos", [128, TC + T], I32)
    flag = din("flag", [128, 1])
    gbias_d = din("gbias", [128, 4 * 8])
    w_in = din("w_in", [D, INC]); w_out = din("w_out", [DMIX, D])
    w_gate = din("w_gate", [D, DFF]); w_up = din("w_up", [D, DFF]); w_down = din("w_down", [DFF, D])
    wpre1_d = din("wpre1", [128, KC]); wpre2_d = din("wpre2", [128, KC])
    wpost1_d = din("wpost1", [128, D]); wpost2_d = din("wpost2", [128, D])
    wpre1b_d = din("wpre1b", [128, D]); wpre2b_d = din("wpre2b", [128, D])
    convw_d = din("convw", [128, NCT * 4]); convb_d = din("convb", [128, NCT])
    dtb_d = din("dtb", [128, NH]); alog_d = din("alog", [128, NH])
    dcol_d = din("dcol", [128, G * 4]); ncol_d = din("ncol", [128, G * 4])
    identb_d = din("identb", [128, 128], BF16); identf_d = din("identf", [128, 128])
    pmT_d = din("pmT", [128, 128]); tri_d = din("tri", [128, 2 * 256])
    maskT_d = din("maskT", [128, 2 * 256], BF16); trib_d = din("trib", [128, 128], BF16)
    sel8_d = din("sel8", [128, 8 * 128]); oh9_d = din("oh9", [128, 9 * 128], BF16)
    invf_d = din("invf", [128, 1])
    out_d = nc.dram_tensor("out", [T, D], F32, kind="ExternalOutput").ap()
    x1_d = nc.dram_tensor("x1_scratch", [T, D], F32, kind="Internal").ap()
    dbg = {}
    if debug:
        for name, shape, dt in debug:
            dbg[name] = nc.dram_tensor("dbg_" + name, list(shape), dt, kind="ExternalOutput").ap()

    K = KB(nc)
    K.off0 = 0
    total = nc.SBUF_PARTITION_SIZE_BYTES
    K.lo = (total - nc.sbuf_bytes_remaining + 63) // 64 * 64
    K.hi = total // 64 * 64 - 64
    K.nc_total = K.hi

    pbt = [nc.alloc_psum_tensor(f"pb{i}", [128, 512], F32) for i in range(8)]
    pb = [Buf(t, f"pb{i}") for i, t in enumerate(pbt)]

    def pf(i):
        return pb[i].t

    def pbf(i):
        return pb[i].t.bitcast(BF16)

    cst = {}

    def cload(name, d_ap, shape, dt=F32, side="R"):
        b = K.sb(name, shape, dt, side)
        K.dma("sp", b.t[:], d_ap, r=(), w=(b,), chan=b)
        cst[name] = b
        return b

    identb = cload("identb", identb_d, [128, 128], BF16)
    identf = cload("identf", identf_d, [128, 128])
    pmT = cload("pmT", pmT_d, [128, 128])
    tri = cload("tri", tri_d.rearrange("p (j l) -> p j l", j=2), [128, 2, 256])
    maskT = cload("maskT", maskT_d.rearrange("p (j l) -> p j l", j=2), [128, 2, 256], BF16)
    trib = cload("trib", trib_d, [128, 128], BF16)
    sel8 = cload("sel8", sel8_d.rearrange("p (h s) -> p h s", h=8), [128, 8, 128])
    oh9 = cload("oh9", oh9_d.rearrange("p (h s) -> p h s", h=9), [128, 9, 128], BF16)
    invf = cload("invf", invf_d, [128, 1])
    flg = cload("flag", flag, [128, 1])
    gbias = cload("gbias", gbias_d.rearrange("p (i j) -> p i j", i=4), [128, 4, 8])
    wpre1 = cload("wpre1", wpre1_d, [128, KC]); wpre2 = cload("wpre2", wpre2_d, [128, KC])
    convw = cload("convw", convw_d.rearrange("p (c k) -> p c k", k=4), [128, NCT, 4])
    convb = cload("convb", convb_d, [128, NCT])
    dtb = cload("dtb", dtb_d, [128, NH]); alog = cload("alog", alog_d, [128, NH])
    dcol = cload("dcol", dcol_d, [128, G * 4]); ncol = cload("ncol", ncol_d, [128, G * 4])
    onesf = K.sb("onesf", [128, 128], F32, "R")
    K.op("dve", lambda e: e.memset(onesf.t[:], 1.0), w=(onesf,))
    onesb = K.sb("onesb", [128, 128], BF16, "R")
    K.op("dve", lambda e: e.memset(onesb.t[:], 1.0), w=(onesb,))
    aneg = K.sb("aneg", [128, NH], F32, "R")
    K.op("act", lambda e: e.activation(out=aneg.t[:], in_=alog.t[:], func=AF.Exp), r=(alog,), w=(aneg,))
    K.op("dve", lambda e: e.tensor_scalar(out=aneg.t[:], in0=aneg.t[:], scalar1=-1.0, scalar2=None, op0=ALU.mult),
         r=(aneg,), w=(aneg,))
    epsc = K.sb("epsc", [128, 1], F32, "R")
    K.op("dve", lambda e: e.memset(epsc.t[:], EPS), w=(epsc,))
    negpi = K.sb("negpi", [128, 1], F32, "R")
    K.op("dve", lambda e: e.memset(negpi.t[:], -float(np.pi)), w=(negpi,))
    onec = K.sb("onec", [128, 1], F32, "R")
    K.op("dve", lambda e: e.memset(onec.t[:], 1.0), w=(onec,))

    if stop == "const":
        K.dma("sp", dbg["ST"][:, 0:128], onesf.t[:], r=(onesf, aneg, onesb, epsc, negpi, onec), w=(), chan=onesf)
        K.wait_all("sp", list(cst.values()) + [onesf]); return nc, dict(peak=K.peak)
    rr = {"ev": 0}

    def ev_eng():
        rr["ev"] ^= 1
        return "act" if rr["ev"] else "dve"

    def copy(en, out, in_, r, w):
        if en == "act":
            K.op("act", lambda e: e.copy(out=out, in_=in_), r=r, w=w)
        else:
            K.op(en, lambda e: e.tensor_copy(out, in_), r=r, w=w)

    def dump(name, buf, ap=None):
        if name in dbg:
            K.dma("sp", dbg[name], buf.t[:] if ap is None else ap, r=(buf,), w=(), chan=buf)

    def rope_tables(col0, n, tag):
        m = K.mark()
        pi_ = K.sb("posi", [128, n], I32)
        K.dma("sp", pi_.t[:], pos[:, col0:col0 + n], w=(pi_,), chan=pi_)
        ang = K.sb("ang", [128, n], F32)
        K.op("dve", lambda e: e.tensor_copy(ang.t[:], pi_.t[:]), r=(pi_,), w=(ang,))
        K.op("dve", lambda e: e.tensor_scalar(out=ang.t[:], in0=ang.t[:], scalar1=invf.t[:, 0:1], scalar2=None,
                                              op0=ALU.mult), r=(ang, invf), w=(ang,))
        tmp = K.sb("angt", [128, n], F32)
        K.release(m)
        cosT = K.sb("cos" + tag, [128, n], F32)
        sinT = K.sb("sin" + tag, [128, n], F32)
        return cosT, sinT

    def make_rope(col0, n, tag):
        cosT = K.sb("cos" + tag, [128, n], F32)
        sinT = K.sb("sin" + tag, [128, n], F32)
        m = K.mark()
        pi_ = K.sb("posi", [128, n], I32)
        K.dma("sp", pi_.t[:], pos[:, col0:col0 + n], w=(pi_,), chan=pi_)
        ang = K.sb("ang", [128, n], F32)
        tmp = K.sb("angt", [128, n], F32)
        K.op("dve", lambda e: e.tensor_copy(ang.t[:], pi_.t[:]), r=(pi_,), w=(ang,))
        K.op("dve", lambda e: e.tensor_scalar(out=ang.t[:], in0=ang.t[:], scalar1=invf.t[:, 0:1], scalar2=None,
                                              op0=ALU.mult), r=(ang, invf), w=(ang,))
        ki = K.sb("angi", [128, n], I32)
        inv2pi = float(1.0 / (2 * np.pi))
        for (dst, sh) in ((sinT, 0.5), (cosT, 0.75)):
            K.op("dve", lambda e: e.tensor_scalar(out=tmp.t[:], in0=ang.t[:], scalar1=inv2pi, scalar2=sh,
                                                  op0=ALU.mult, op1=ALU.add), r=(ang,), w=(tmp,))
            K.op("dve", lambda e: e.tensor_copy(ki.t[:], tmp.t[:]), r=(tmp,), w=(ki,))
            K.op("dve", lambda e: e.tensor_copy(dst.t[:], ki.t[:]), r=(ki,), w=(dst,))
            K.op("dve", lambda e: e.tensor_tensor(out=tmp.t[:], in0=tmp.t[:], in1=dst.t[:], op=ALU.subtract),
                 r=(tmp, dst), w=(tmp,))
            K.op("dve", lambda e: e.tensor_scalar(out=dst.t[:], in0=tmp.t[:], scalar1=0.0, scalar2=None,
                                                  op0=ALU.is_lt), r=(tmp,), w=(dst,))
            K.op("dve", lambda e: e.tensor_tensor(out=tmp.t[:], in0=tmp.t[:], in1=dst.t[:], op=ALU.add),
                 r=(tmp, dst), w=(tmp,))
            K.op("act", lambda e: e.activation(out=dst.t[:], in_=tmp.t[:], func=AF.Sin, bias=negpi.t[:, 0:1],
                                               scale=float(2 * np.pi)), r=(tmp, negpi), w=(dst,))
        K.wait_all("dve", [ki])
        K.wait_all("dve", [tmp, ang, pi_])
        K.wait_all("act", [tmp])
        K.wait_all("sp", [pi_])
        K.release(m)
        return cosT, sinT

    hbank = {"n": 0}

    def build_hT(x_d, ntok, hT, wb_d, rbuf=()):
        m = K.mark()
        wb = K.sb("wbc", [128, D], F32)
        K.dma("sp", wb.t[:], wb_d, w=(wb,), chan=wb)
        xt = [K.sb(f"xt{i}", [128, D], F32) for i in range(2)]
        hb = [K.sb(f"hb{i}", [128, D], BF16) for i in range(2)]
        st = [K.sb(f"st{i}", [128, 2], F32) for i in range(2)]
        ntile = ntok // 128

        def stageA(tt):
            x_, h_, s_ = xt[tt % 2], hb[tt % 2], st[tt % 2]
            K.dma("sp", x_.t[:], x_d[tt * 128:(tt + 1) * 128, :], r=rbuf, w=(x_,), chan=x_)
            K.op("act", lambda e: e.activation(out=h_.t[:], in_=x_.t[:], func=AF.Square, accum_out=s_.t[:, 0:1]),
                 r=(x_,), w=(h_, s_))
            K.op("act", lambda e: e.activation(out=s_.t[:, 1:2], in_=s_.t[:, 0:1], func=AF.Sqrt, bias=epsc.t[:, 0:1],
                                               scale=1.0 / D), r=(s_, epsc), w=(s_,))
            K.op("dve", lambda e: e.reciprocal(out=s_.t[:, 1:2], in_=s_.t[:, 1:2]), r=(s_,), w=(s_,))
            K.op("dve", lambda e: e.scalar_tensor_tensor(out=h_.t[:], in0=x_.t[:], scalar=s_.t[:, 1:2], in1=wb.t[:],
                                                         op0=ALU.mult, op1=ALU.mult), r=(x_, s_, wb), w=(h_,))

        def stageB(tt):
            h_ = hb[tt % 2]
            for k8 in range(0, KC, 8):
                bank = hbank["n"] % 8
                hbank["n"] += 1
                nk = min(8, KC - k8)
                for j in range(nk):
                    kc = k8 + j
                    K.op("pe", lambda e: e.transpose(out=pbf(bank)[:, j * 128:(j + 1) * 128],
                                                     in_=h_.t[:, kc * 128:(kc + 1) * 128], identity=identb.t[:]),
                         r=(h_, identb), w=(pb[bank],), inc=(j == nk - 1), acc=True)
                copy(ev_eng(), hT.t[:, k8:k8 + nk, tt * 128:(tt + 1) * 128],
                     pbf(bank)[:, 0:nk * 128].rearrange("p (k t) -> p k t", k=nk), r=(pb[bank],), w=(hT,))
        stageA(0)
        for tt in range(ntile):
            if tt + 1 < ntile:
                stageA(tt + 1)
            stageB(tt)
        K.release(m)

    class Ring:
        def __init__(self, n, width, name):
            self.bufs = [K.sb(f"{name}{i}", [128, 8, width], BF16) for i in range(n)]
            self.i = 0
            self.width = width

        def load(self, w_d, r0, nk, cc0, ncols):
            b = self.bufs[self.i % len(self.bufs)]
            self.i += 1
            K.dma("pool", b.t[:, 0:nk, 0:ncols],
                  w_d[r0:r0 + nk * 128, cc0:cc0 + ncols].rearrange("(k p) n -> p k n", p=128), w=(b,), chan=b)
            return b

        def drain(self):
            for e_ in ("pool", "pe"):
                K.wait_all(e_, self.bufs)

    def gemm_B(ring, w_d, cc0, ncols, xT, tok0, ntok, banks, cb, nK=None, mcols=None):
        nK = KC if nK is None else nK
        nct = (ncols + 127) // 128
        nh = ntok // 512
        slabs = []
        for s0 in range(0, nK, 8):
            nk = min(8, nK - s0)
            slabs.append((ring.load(w_d, s0 * 128, nk, cc0, ncols), s0, nk))
        for ct in range(nct):
            mc = min(128, ncols - ct * 128)
            bs = banks[ct * nh:(ct + 1) * nh]
            for (sl, s0, nk) in slabs:
                for j in range(nk):
                    kc = s0 + j
                    for h in range(nh):
                        last = (kc == nK - 1)
                        K.op("pe", lambda e: e.matmul(pf(bs[h])[0:mc, :], lhsT=sl.t[:, j, ct * 128:ct * 128 + mc],
                                                      rhs=xT.t[:, kc, tok0 + h * 512:tok0 + (h + 1) * 512],
                                                      start=(kc == 0), stop=last),
                             r=(sl, xT), w=(pb[bs[h]],), inc=(last or (j == nk - 1 and h == nh - 1)), acc=True)
            cb(ct, bs)

    def conv_silu(bs, ci, tail_in, tail_out, outs, scratch):
        xraw, acc = scratch
        if tail_in is None:
            K.op("dve", lambda e: e.memset(xraw.t[:, 0:3], 0.0), w=(xraw,))
        else:
            tb, tap = tail_in
            K.op("dve", lambda e: e.tensor_copy(xraw.t[:, 0:3], tap), r=(tb,), w=(xraw,))
        for h in range(2):
            copy("act", xraw.t[:, 3 + h * 512:3 + (h + 1) * 512], pf(bs[h])[:, :], r=(pb[bs[h]],), w=(xraw,))
        if tail_out is not None:
            tb, tap = tail_out
            K.op("dve", lambda e: e.tensor_copy(tap, xraw.t[:, 1024:1027]), r=(xraw,), w=(tb,))
        K.op("dve", lambda e: e.tensor_scalar(out=acc.t[:], in0=xraw.t[:, 3:1027], scalar1=convw.t[:, ci, 3:4],
                                              scalar2=convb.t[:, ci:ci + 1], op0=ALU.mult, op1=ALU.add),
             r=(xraw, convw, convb), w=(acc,))
        for k in range(3):
            K.op("dve", lambda e: e.scalar_tensor_tensor(out=acc.t[:], in0=xraw.t[:, k:k + 1024],
                                                         scalar=convw.t[:, ci, k:k + 1], in1=acc.t[:],
                                                         op0=ALU.mult, op1=ALU.add),
                 r=(xraw, convw, acc), w=(acc,))
        for (ap, b) in outs:
            K.op("act", lambda e: e.activation(out=ap, in_=acc.t[:], func=AF.Silu), r=(acc,), w=(b,))

    def rope_head(bs, cosT, sinT, raw, ro, rb=(6, 7)):
        for h in range(2):
            copy("act", raw.t[:, h * 512:(h + 1) * 512], pf(bs[h])[:, :], r=(pb[bs[h]],), w=(raw,))
        for h in range(2):
            sl = slice(h * 512, (h + 1) * 512)
            K.op("pe", lambda e: e.matmul(pf(rb[h])[:, :], lhsT=pmT.t[:], rhs=raw.t[:, sl], start=True, stop=True),
                 r=(pmT, raw), w=(pb[rb[h]],))
            K.op("dve", lambda e: e.tensor_tensor(out=ro.t[:, sl], in0=pf(rb[h])[:, :], in1=sinT.t[:, sl], op=ALU.mult),
                 r=(pb[rb[h]], sinT), w=(ro,))
        K.op("dve", lambda e: e.tensor_tensor(out=raw.t[:], in0=raw.t[:], in1=cosT.t[:], op=ALU.mult),
             r=(raw, cosT), w=(raw,))
        K.op("dve", lambda e: e.tensor_tensor(out=ro.t[:], in0=ro.t[:], in1=raw.t[:], op=ALU.add),
             r=(ro, raw), w=(ro,))

    def transpose_bf(src, src_ap_fn, ntiles, dst, dst_ap_fn, bank):
        for i0 in range(0, ntiles, 8):
            n = min(8, ntiles - i0)
            for j in range(n):
                K.op("pe", lambda e: e.transpose(out=pbf(bank)[:, j * 128:(j + 1) * 128], in_=src_ap_fn(i0 + j),
                                                 identity=identb.t[:]),
                     r=(src, identb), w=(pb[bank],), inc=(j == n - 1), acc=True)
            for j in range(n):
                copy(ev_eng(), dst_ap_fn(i0 + j), pbf(bank)[:, j * 128:(j + 1) * 128], r=(pb[bank],), w=(dst,))

    def dt_prep(ring, hT, P):
        m = K.mark()
        dtr = K.sb("dtr", [128, 1024], F32)

        def cb(ct, bs):
            for h in range(2):
                copy(ev_eng(), dtr.t[0:NH, h * 512:(h + 1) * 512], pf(bs[h])[0:NH, :], r=(pb[bs[h]],), w=(dtr,))
        gemm_B(ring, w_in, dt0, NH, hT, 0, 1024, [0, 1], cb)
        tmp = K.sb("dtt", [128, NT, NH], F32)
        tmp2 = K.sb("dtt2", [128, NT, NH], F32)
        adt = K.sb("adt", [128, NT, NH], F32)
        acum = K.sb("acum", [128, NT, NH], F32)
        tot = K.sb("tot", [128, 4, NH], F32)
        dt, dtsd, nacum, cd = P["dt"], P["dtsd"], P["nacum"], P["cd"]
        for tt in range(NT):
            K.op("pe", lambda e: e.transpose(out=pf(4)[:, tt * NH:(tt + 1) * NH], in_=dtr.t[0:NH, tt * 128:(tt + 1) * 128],
                                             identity=identf.t[0:NH, 0:NH]), r=(dtr, identf), w=(pb[4],),
                 inc=(tt == NT - 1), acc=True)
        p4 = pf(4)[:, 0:NT * NH].rearrange("p (t h) -> p t h", t=NT)
        K.op("dve", lambda e: e.tensor_tensor(out=tmp.t[:], in0=p4,
                                              in1=dtb.t[:].unsqueeze(1).broadcast_to([128, NT, NH]), op=ALU.add),
             r=(pb[4], dtb), w=(tmp,))
        K.op("act", lambda e: e.activation(out=tmp2.t[:], in_=tmp.t[:], func=AF.Abs), r=(tmp,), w=(tmp2,))
        K.op("act", lambda e: e.activation(out=tmp2.t[:], in_=tmp2.t[:], func=AF.Exp, scale=-1.0), r=(tmp2,), w=(tmp2,))
        K.op("act", lambda e: e.activation(out=tmp2.t[:], in_=tmp2.t[:], func=AF.Ln, bias=onec.t[:, 0:1], scale=1.0),
             r=(tmp2, onec), w=(tmp2,))
        K.op("dve", lambda e: e.scalar_tensor_tensor(out=dt.t[:], in0=tmp.t[:], scalar=0.0, in1=tmp2.t[:],
                                                     op0=ALU.max, op1=ALU.add), r=(tmp, tmp2), w=(dt,))
        K.op("dve", lambda e: e.tensor_tensor(out=adt.t[:], in0=dt.t[:],
                                              in1=aneg.t[:].unsqueeze(1).broadcast_to([128, NT, NH]), op=ALU.mult),
             r=(dt, aneg), w=(adt,))
        for c in range(4):
            for lt in range(2):
                for j in range(2):
                    K.op("pe", lambda e: e.matmul(pf(5)[:, (c * 2 + lt) * NH:(c * 2 + lt + 1) * NH],
                                                  lhsT=tri.t[:, j, lt * 128:(lt + 1) * 128], rhs=adt.t[:, 2 * c + j, :],
                                                  start=(j == 0), stop=(j == 1)),
                         r=(tri, adt), w=(pb[5],), inc=(c == 3 and lt == 1 and j == 1), acc=True)
            for j in range(2):
                K.op("pe", lambda e: e.matmul(pf(6)[:, c * NH:(c + 1) * NH], lhsT=onesf.t[:], rhs=adt.t[:, 2 * c + j, :],
                                              start=(j == 0), stop=(j == 1)),
                     r=(onesf, adt), w=(pb[6],), inc=(c == 3 and j == 1), acc=True)
        copy("dve", acum.t[:], pf(5)[:, 0:NT * NH].rearrange("p (t h) -> p t h", t=NT), r=(pb[5],), w=(acum,))
        copy("dve", tot.t[:], pf(6)[:, 0:4 * NH].rearrange("p (c h) -> p c h", c=4), r=(pb[6],), w=(tot,))
        K.op("act", lambda e: e.activation(out=cd.t[:], in_=tot.t[:], func=AF.Exp), r=(tot,), w=(cd,))
        K.op("dve", lambda e: e.tensor_scalar(out=nacum.t[:], in0=acum.t[:], scalar1=-1.0, scalar2=None, op0=ALU.mult),
             r=(acum,), w=(nacum,))
        for c in range(4):
            K.op("dve", lambda e: e.tensor_tensor(out=tmp.t[:, 2 * c:2 * c + 2, :], in0=nacum.t[:, 2 * c:2 * c + 2, :],
                                                  in1=tot.t[:, c:c + 1, :].broadcast_to([128, 2, NH]), op=ALU.add),
                 r=(nacum, tot), w=(tmp,))
        K.op("act", lambda e: e.activation(out=tmp.t[:], in_=tmp.t[:], func=AF.Exp), r=(tmp,), w=(tmp,))
        K.op("dve", lambda e: e.tensor_tensor(out=dtsd.t[:], in0=tmp.t[:], in1=dt.t[:], op=ALU.mult),
             r=(tmp, dt), w=(dtsd,))
        for e_ in ("act", "dve", "pe"):
            K.wait_all(e_, [dtr, tmp, tmp2, adt, acum, tot])
        K.release(m)

    def make_acumT(P, g):
        dst = P["acumT"][g]
        for tt in range(NT):
            K.op("pe", lambda e: e.transpose(out=pf(7)[0:8, (tt % 4) * 128:(tt % 4 + 1) * 128],
                                             in_=P["nacum"].t[:, tt, g * 8:(g + 1) * 8], identity=identf.t[:]),
                 r=(P["nacum"], identf), w=(pb[7],), inc=(tt % 4 == 3), acc=True)
            if tt % 4 == 3:
                K.op("act", lambda e: e.activation(out=dst.t[0:8, (tt // 4) * 512:(tt // 4 + 1) * 512], in_=pf(7)[0:8, 0:512],
                                                   func=AF.Copy, scale=-1.0), r=(pb[7],), w=(dst,))

    mL0 = K.mark()
    ST = K.sb("ST", [128, NH * 64], F32)
    K.op("dve", lambda e: e.memset(ST.t[:], 0.0), w=(ST,))
    tails = K.sb("tails", [128, NCT, 3], F32)
    kT = [K.sb(f"kT{h}", [128, TC], BF16) for h in range(NKV)]
    vtok = [K.sb(f"vtok{h}", [128, NT, 132], BF16) for h in range(NKV)]
    for h in range(NKV):
        K.op("dve", lambda e: e.memset(vtok[h].t[:, :, 128:132], 1.0), w=(vtok[h],))
    kms = K.sb("kms", [128, NKV, 8], F32)
    kmax = K.sb("kmax", [128, NKV, 4], F32)
    mP1 = K.mark()

    def kv_phase(hT, blk_base, cosT, sinT, ring, scr, heads, kdst, vdst, rb=(6, 7)):
        raw, ro, vT = scr
        sq = raw
        for h in heads:
            def cbk(ct, bs, h=h):
                rope_head(bs, cosT, sinT, raw, ro, rb)
                copy("act", kdst[h].t[:, 0:1024], ro.t[:], r=(ro,), w=(kdst[h],))
                bi = blk_base
                K.op("dve", lambda e: e.tensor_reduce(out=kms.t[:, h, bi:bi + 4],
                                                      in_=ro.t[:].rearrange("p (b t) -> p b t", b=4),
                                                      axis=AX.X, op=ALU.add), r=(ro,), w=(kms,))
                K.op("act", lambda e: e.activation(out=sq.t[:], in_=ro.t[:], func=AF.Square), r=(ro,), w=(sq,))
                for hh in range(2):
                    K.op("pe", lambda e: e.matmul(pf(rb[hh])[:, :], lhsT=onesf.t[:], rhs=sq.t[:, hh * 512:(hh + 1) * 512],
                                                  start=True, stop=True), r=(onesf, sq), w=(pb[rb[hh]],))
                    pi = blk_base // 2 + hh
                    K.op("dve", lambda e: e.tensor_reduce(out=kmax.t[:, h, pi:pi + 1], in_=pf(rb[hh])[:, :],
                                                          axis=AX.X, op=ALU.max), r=(pb[rb[hh]],), w=(kmax,))
            gemm_B(ring, w_in, k0 + h * 128, 128, hT, 0, 1024, [0, 1], cbk)

            def cbv(ct, bs, h=h):
                for hh in range(2):
                    copy(ev_eng(), vT.t[:, hh * 512:(hh + 1) * 512], pf(bs[hh])[:, :], r=(pb[bs[hh]],), w=(vT,))
                transpose_bf(vT, lambda i: vT.t[:, i * 128:(i + 1) * 128], NT, vdst[h],
                             lambda i: vdst[h].t[:, i, 0:128], 5)
            gemm_B(ring, w_in, v0 + h * 128, 128, hT, 0, 1024, [2, 3], cbv)

    def ssd_state_update(c, ti, Btok, xdtsd, cd, bank=6):
        h0 = ti * 2
        for j in range(2):
            K.op("pe", lambda e: e.matmul(pf(bank)[:, 256:384], lhsT=Btok.t[:, 2 * c + j, :], rhs=xdtsd.t[:, 2 * c + j, :],
                                          start=(j == 0), stop=(j == 1)), r=(Btok, xdtsd), w=(pb[bank],), inc=(j == 1), acc=True)
        sl = ST.t[:, ti * 128:(ti + 1) * 128].rearrange("p (h d) -> p h d", h=2)
        K.op("dve", lambda e: e.tensor_tensor(out=sl, in0=sl,
                                              in1=cd.t[:, c, h0:h0 + 2].unsqueeze(2).broadcast_to([128, 2, 64]),
                                              op=ALU.mult), r=(ST, cd), w=(ST,))
        K.op("dve", lambda e: e.tensor_tensor(out=ST.t[:, ti * 128:(ti + 1) * 128], in0=ST.t[:, ti * 128:(ti + 1) * 128],
                                              in1=pf(bank)[:, 256:384], op=ALU.add), r=(ST, pb[bank]), w=(ST,))

    def xs_tile_prep(ring, hT, ti, tail_in, tail_out, P, scr, xsf, xdt, xdtsd):
        xsb, xtok = scr["xsb"], scr["xtok"]
        h0 = ti * 2

        def cb(ct, bs):
            outs = [(xsb.t[:], xsb)]
            if xsf is not None:
                outs.append((xsf.t[:], xsf))
            conv_silu(bs, ti, tail_in, tail_out, outs, (scr["xraw"], scr["acc"]))
        gemm_B(ring, w_in, x0 + ti * 128, 128, hT, 0, 1024, [0, 1], cb)
        transpose_bf(xsb, lambda i: xsb.t[:, i * 128:(i + 1) * 128], NT, xtok, lambda i: xtok.t[:, i, :], 5)
        x4 = xtok.t[:].rearrange("p t (h d) -> p t h d", h=2)
        if xdt is not None:
            K.op("dve", lambda e: e.tensor_tensor(out=xdt.t[:].rearrange("p t (h d) -> p t h d", h=2), in0=x4,
                                                  in1=P["dt"].t[:, :, h0:h0 + 2].unsqueeze(3).broadcast_to([128, NT, 2, 64]),
                                                  op=ALU.mult), r=(xtok, P["dt"]), w=(xdt,))
        K.op("dve", lambda e: e.tensor_tensor(out=xdtsd.t[:].rearrange("p t (h d) -> p t h d", h=2), in0=x4,
                                              in1=P["dtsd"].t[:, :, h0:h0 + 2].unsqueeze(3).broadcast_to([128, NT, 2, 64]),
                                              op=ALU.mult), r=(xtok, P["dtsd"]), w=(xdtsd,))

    def bc_tile(ring, hT, ci, tail_in, tail_out, outT, scr):
        def cb(ct, bs):
            conv_silu(bs, ci, tail_in, tail_out, [(outT.t[:], outT)] if outT is not None else [], (scr["xraw"], scr["acc"]))
        col = (b0 + (ci - 4 * G) * 128) if ci < 5 * G else (c0 + (ci - 5 * G) * 128)
        gemm_B(ring, w_in, col, 128, hT, 0, 1024, [2, 3], cb)

    hT_ctx = K.sb("hT_ctx", [128, KC, TC], BF16)
    build_hT(x_ctx, TC, hT_ctx, wpre1b_d)
    dump("hT_ctx", hT_ctx)
    if stop == "hT":
        K.wait_all("sp", [hT_ctx]); return nc, dict(peak=K.peak)
    ring = Ring(4, 128, "slab")
    cosC, sinC = make_rope(0, TC, "C")
    scr_kv = (K.sb("kraw", [128, 1024], F32), K.sb("kro", [128, 1024], F32), K.sb("vT", [128, 1024], BF16))
    kv_phase(hT_ctx, 0, cosC, sinC, ring, scr_kv, list(range(NKV)), kT, vtok)
    dump("kT0", kT[0], kT[0].t[:, 0:1024])
    if stop == "kv":
        K.wait_all("sp", [hT_ctx, kT[0]]); return nc, dict(peak=K.peak)
    P = {n: K.sb(n + "C", [128, NT, NH], F32) for n in ("dt", "dtsd", "nacum")}
    P["cd"] = K.sb("cdC", [128, 4, NH], F32)
    dt_prep(ring, hT_ctx, P)
    if stop == "dt":
        K.wait_all("sp", [hT_ctx, kT[0]]); return nc, dict(peak=K.peak)
    scrs = [{"xsb": K.sb(f"xsb{i}", [128, 1024], BF16), "xtok": K.sb(f"xtok{i}", [128, NT, 128], BF16),
             "xraw": K.sb(f"xraw{i}", [128, 1027], F32), "acc": K.sb(f"acc{i}", [128, 1024], F32)} for i in range(2)]
    xdtsds = [K.sb(f"xdtsd{i}", [128, NT, 128], BF16) for i in range(2)]
    BTs = [K.sb(f"BT{i}", [128, 1024], BF16) for i in range(2)]
    Btoks = [K.sb(f"Btok{i}", [128, NT, 128], BF16) for i in range(2)]
    tiles1 = [(g, hp) for g in range(G) for hp in range(4)]

    def emit_BC(g):
        sc = scrs[g % 2]
        BT_, Bk_ = BTs[g % 2], Btoks[g % 2]
        bc_tile(ring, hT_ctx, 4 * G + g, None, (tails, tails.t[:, 4 * G + g, :]), BT_, sc)
        transpose_bf(BT_, lambda i: BT_.t[:, i * 128:(i + 1) * 128], NT, Bk_, lambda i: Bk_.t[:, i, :], 5)
        bc_tile(ring, hT_ctx, 5 * G + g, None, (tails, tails.t[:, 5 * G + g, :]), None, sc)

    def gemm_xs1(t):
        g, hp = tiles1[t]
        ti = g * 4 + hp
        sc = scrs[t % 2]

        def cb(ct, bs):
            conv_silu(bs, ti, None, (tails, tails.t[:, ti, :]), [(sc["xsb"].t[:], sc["xsb"])], (sc["xraw"], sc["acc"]))
        gemm_B(ring, w_in, x0 + ti * 128, 128, hT_ctx, 0, 1024, [0, 1], cb)

    def post1(t):
        g, hp = tiles1[t]
        ti = g * 4 + hp
        h0 = ti * 2
        sc = scrs[t % 2]
        xd = xdtsds[t % 2]
        xsb, xtok = sc["xsb"], sc["xtok"]
        transpose_bf(xsb, lambda i: xsb.t[:, i * 128:(i + 1) * 128], NT, xtok, lambda i: xtok.t[:, i, :], 5)
        K.op("dve", lambda e: e.tensor_tensor(out=xd.t[:].rearrange("p t (h d) -> p t h d", h=2),
                                              in0=xtok.t[:].rearrange("p t (h d) -> p t h d", h=2),
                                              in1=P["dtsd"].t[:, :, h0:h0 + 2].unsqueeze(3).broadcast_to([128, NT, 2, 64]),
                                              op=ALU.mult), r=(xtok, P["dtsd"]), w=(xd,))
        for c in range(4):
            ssd_state_update(c, ti, Btoks[g % 2], xd, P["cd"], bank=6 + c % 2)

    emit_BC(0)
    gemm_xs1(0)
    for t in range(len(tiles1)):
        if t + 1 < len(tiles1):
            if tiles1[t + 1][1] == 0:
                emit_BC(tiles1[t + 1][0])
            gemm_xs1(t + 1)
        post1(t)
    K.op("dve", lambda e: e.tensor_scalar(out=ST.t[:], in0=ST.t[:], scalar1=flg.t[:, 0:1], scalar2=None, op0=ALU.mult),
         r=(ST, flg), w=(ST,))
    dump("ST", ST)
    ring.drain()
    for e_ in ("pe", "act", "dve", "pool", "sp"):
        K.wait_all(e_, [hT_ctx, cosC, sinC] + list(scr_kv) + list(P.values()))
    K.release(mP1)

    mR = K.mark()
    mixS = K.sb("mixS", [128, MC - AC, T], BF16, "R")
    hT_own = K.sb("hT_own", [128, KC, T], BF16)
    build_hT(x_own, T, hT_own, wpre1b_d)
    mP2 = K.mark()
    ring = Ring(4, 128, "slab")
    P = {n: K.sb(n + "O", [128, NT, NH], F32) for n in ("dt", "dtsd", "nacum")}
    P["cd"] = K.sb("cdO", [128, 4, NH], F32)
    acumT1 = K.sb("acumT", [128, 1024], F32)
    P["acumT"] = [acumT1 for g in range(G)]
    dt_prep(ring, hT_own, P)
    scr = {"xsb": K.sb("xsb", [128, 1024], BF16),
           "xraw": K.sb("xraw", [128, 1027], F32), "acc": K.sb("acc", [128, 1024], F32)}
    xdt = K.sb("xdt", [128, NT, 128], BF16)
    xdtsd = K.sb("xdtsd", [128, NT, 128], BF16)
    BT = K.sb("BT", [128, 1024], BF16); CT = K.sb("CT", [128, 1024], BF16)
    Btok = K.sb("Btok", [128, NT, 128], BF16)
    xsf = K.sb("xsf", [128, 1024], F32); zs = scr["xraw"]
    yg = K.sb("yg", [128, 4, 1024], F32)
    STbp = [K.sb(f"STbp{i}", [128, 128], BF16) for i in range(2)]
    cbms = [K.sb(f"cbm{i}", [128, 2, 256], F32) for i in range(2)]
    m01 = K.sb("m01", [128, 2, 256], BF16)
    K.op("dve", lambda e: e.tensor_scalar(out=m01.t[:], in0=maskT.t[:], scalar1=-1.0, scalar2=None, op0=ALU.is_ge),
         r=(maskT,), w=(m01,))
    decs = [K.sb(f"dec{i}", [128, 2, 256], F32) for i in range(2)]
    MTs = [[K.sb(f"MT{p}{i}", [128, 2, 256], BF16) for i in range(2)] for p in range(2)]
    eas = [K.sb(f"ea{i}", [128, 256], F32) for i in range(2)]
    Css = [[K.sb(f"Cs{p}{i}", [128, 256], BF16) for i in range(2)] for p in range(2)]
    rs = scr["acc"]

    class _Stop(Exception):
        pass

    chkc = {}

    def chk(name):
        chkc[name] = chkc.get(name, 0) + 1
        if stop == name or stop == f"{name}@{chkc[name]}":
            raise _Stop()
    try:
      for g in range(G):
          make_acumT(P, g)
          bc_tile(ring, hT_own, 4 * G + g, (tails, tails.t[:, 4 * G + g, :]), None, BT, scr)
          transpose_bf(BT, lambda i: BT.t[:, i * 128:(i + 1) * 128], NT, Btok, lambda i: Btok.t[:, i, :], 5)
          bc_tile(ring, hT_own, 5 * G + g, (tails, tails.t[:, 5 * G + g, :]), None, CT, scr)
          for hp in range(4):
              ti = g * 4 + hp
              h0 = ti * 2

              def cbx(ct, bs, ti=ti):
                  conv_silu(bs, ti, (tails, tails.t[:, ti, :]), None, [(scr["xsb"].t[:], scr["xsb"]), (xsf.t[:], xsf)],
                            (scr["xraw"], scr["acc"]))
              gemm_B(ring, w_in, x0 + ti * 128, 128, hT_own, 0, 1024, [0, 1], cbx)

              def cbz(ct, bs):
                  for hh in range(2):
                      K.op("act", lambda e: e.activation(out=zs.t[:, hh * 512:(hh + 1) * 512], in_=pf(bs[hh])[:, :],
                                                         func=AF.Silu), r=(pb[bs[hh]],), w=(zs,))
              gemm_B(ring, w_in, z0 + ti * 128, 128, hT_own, 0, 1024, [0, 1], cbz)
              xsb = scr["xsb"]
              for i8 in range(NT):
                  K.op("pe", lambda e: e.transpose(out=pbf(5)[:, i8 * 128:(i8 + 1) * 128], in_=xsb.t[:, i8 * 128:(i8 + 1) * 128],
                                                   identity=identb.t[:]), r=(xsb, identb), w=(pb[5],), inc=(i8 == NT - 1), acc=True)
              x4 = pbf(5)[:, :].rearrange("p (t h d) -> p t h d", t=NT, h=2)
              K.op("dve", lambda e: e.tensor_tensor(out=xdt.t[:].rearrange("p t (h d) -> p t h d", h=2), in0=x4,
                                                    in1=P["dt"].t[:, :, h0:h0 + 2].unsqueeze(3).broadcast_to([128, NT, 2, 64]),
                                                    op=ALU.mult), r=(pb[5], P["dt"]), w=(xdt,))
              K.op("dve", lambda e: e.tensor_tensor(out=xdtsd.t[:].rearrange("p t (h d) -> p t h d", h=2), in0=x4,
                                                    in1=P["dtsd"].t[:, :, h0:h0 + 2].unsqueeze(3).broadcast_to([128, NT, 2, 64]),
                                                    op=ALU.mult), r=(pb[5], P["dtsd"]), w=(xdtsd,))
              copy("act", STbp[0].t[:], ST.t[:, ti * 128:(ti + 1) * 128], r=(ST,), w=(STbp[0],))

              def stage1(c, hp=hp, ti=ti, g=g):
                  cs = slice(c * 256, (c + 1) * 256)
                  cb_ = cbms[c % 2]
                  for j in range(2):
                      K.op("pe", lambda e: e.matmul(pf(4)[:, j * 256:(j + 1) * 256], lhsT=BT.t[:, (2 * c + j) * 128:(2 * c + j + 1) * 128],
                                                    rhs=CT.t[:, cs], start=True, stop=True), r=(BT, CT), w=(pb[4],), inc=(j == 1), acc=True)
                  copy("act", cb_.t[:], pf(4)[:, :].rearrange("p (j l) -> p j l", j=2), r=(pb[4],), w=(cb_,))
                  K.op("dve", lambda e: e.tensor_tensor(out=cb_.t[:], in0=cb_.t[:], in1=m01.t[:], op=ALU.mult), r=(cb_, m01), w=(cb_,))
                  for e2 in range(2):
                      lh = hp * 2 + e2
                      K.op("pe", lambda e: e.matmul(pf(5)[:, e2 * 256:(e2 + 1) * 256], lhsT=sel8.t[0:8, lh, :],
                                                    rhs=P["acumT"][g].t[0:8, cs], start=True, stop=True),
                           r=(sel8, P["acumT"][g]), w=(pb[5],), inc=(e2 == 1), acc=True)
                  for e2 in range(2):
                      Rv = pf(5)[:, e2 * 256:(e2 + 1) * 256]
                      K.op("act", lambda e: e.activation(out=eas[e2].t[:], in_=Rv, func=AF.Exp), r=(pb[5],), w=(eas[e2],))
                  for e2 in range(2):
                      h = ti * 2 + e2
                      Rv = pf(5)[:, e2 * 256:(e2 + 1) * 256]
                      MT_, Cs_ = MTs[c % 2][e2], Css[c % 2][e2]
                      for j in range(2):
                          K.op("dve", lambda e: e.tensor_scalar(out=decs[e2].t[:, j, :], in0=Rv, scalar1=P["nacum"].t[:, 2 * c + j, h:h + 1],
                                                                scalar2=0.0, op0=ALU.add, op1=ALU.min),
                               r=(pb[5], P["nacum"], eas[0], eas[1]), w=(decs[e2],))
                      K.op("act", lambda e: e.activation(out=decs[e2].t[:], in_=decs[e2].t[:], func=AF.Exp), r=(decs[e2],), w=(decs[e2],))
                      K.op("dve", lambda e: e.tensor_tensor(out=Cs_.t[:], in0=CT.t[:, cs], in1=eas[e2].t[:], op=ALU.mult),
                           r=(CT, eas[e2]), w=(Cs_,))

              def stage1b(c):
                  cb_ = cbms[c % 2]
                  for e2 in range(2):
                      MT_ = MTs[c % 2][e2]
                      K.op("dve", lambda e: e.tensor_tensor(out=MT_.t[:], in0=cb_.t[:], in1=decs[e2].t[:], op=ALU.mult),
                           r=(cb_, decs[e2]), w=(MT_,))

              def stage2(c, hp=hp, ti=ti):
                  cs = slice(c * 256, (c + 1) * 256)
                  for e2 in range(2):
                      MT_, Cs_ = MTs[c % 2][e2], Css[c % 2][e2]
                      yb = pf(6 + e2)[:, 0:256]
                      for j in range(2):
                          K.op("pe", lambda e: e.matmul(yb, lhsT=xdt.t[:, 2 * c + j, :], rhs=MT_.t[:, j, :], start=(j == 0), stop=False),
                               r=(xdt, MT_), w=(pb[6 + e2],), inc=False, acc=True)
                      K.op("pe", lambda e: e.matmul(yb, lhsT=STbp[c % 2].t[:], rhs=Cs_.t[:], start=False, stop=True),
                           r=(STbp[c % 2], Cs_), w=(pb[6 + e2],), acc=True)
                      rows = slice(e2 * 64, (e2 + 1) * 64)
                      K.op("dve", lambda e: e.scalar_tensor_tensor(out=yg.t[rows, hp, cs], in0=xsf.t[rows, cs],
                                                                   scalar=dcol.t[rows, ti:ti + 1], in1=pf(6 + e2)[rows, 0:256],
                                                                   op0=ALU.mult, op1=ALU.add), r=(xsf, dcol, pb[6 + e2]), w=(yg,))

              def state(c, ti=ti):
                  ssd_state_update(c, ti, Btok, xdtsd, P["cd"], bank=0)
                  copy("act", STbp[(c + 1) % 2].t[:], ST.t[:, ti * 128:(ti + 1) * 128], r=(ST,), w=(STbp[(c + 1) % 2],))

              stage1(0)
              stage1b(0)
              for c in range(4):
                  if c + 1 < 4:
                      stage1(c + 1)
                  stage2(c)
                  state(c)
                  if c + 1 < 4:
                      stage1b(c + 1)
              chk("s6")
              K.op("dve", lambda e: e.tensor_tensor(out=yg.t[:, hp, :], in0=yg.t[:, hp, :], in1=zs.t[:, 0:1024], op=ALU.mult),
                   r=(yg, zs), w=(yg,))
              K.op("act", lambda e: e.activation(out=scr["acc"].t[:], in_=yg.t[:, hp, :], func=AF.Square), r=(yg,), w=(scr["acc"],))
              for hh in range(2):
                  K.op("pe", lambda e: e.matmul(pf(2 + hh)[:, :], lhsT=onesf.t[:], rhs=scr["acc"].t[:, hh * 512:(hh + 1) * 512],
                                                start=(hp == 0), stop=(hp == 3)), r=(onesf, scr["acc"]), w=(pb[2 + hh],), acc=True)
          for hh in range(2):
              K.op("act", lambda e: e.activation(out=rs.t[:, hh * 512:(hh + 1) * 512], in_=pf(2 + hh)[:, :], func=AF.Sqrt,
                                                 bias=epsc.t[:, 0:1], scale=1.0 / 512), r=(pb[2 + hh], epsc), w=(rs,))
          K.op("dve", lambda e: e.reciprocal(out=rs.t[:], in_=rs.t[:]), r=(rs,), w=(rs,))
          for hp in range(4):
              ti = g * 4 + hp
              K.op("dve", lambda e: e.scalar_tensor_tensor(out=mixS.t[:, ti, :], in0=yg.t[:, hp, :], scalar=ncol.t[:, ti:ti + 1],
                                                           in1=rs.t[:], op0=ALU.mult, op1=ALU.mult), r=(yg, ncol, rs), w=(mixS,))
    except _Stop:
        K.wait_all("sp", [ST, kT[0], hT_ctx])
        for en in ("pe", "act", "dve", "pool", "sp"):
            K.wait_all(en, K.live)
        return nc, dict(peak=K.peak)
    dump("ssmT", mixS)
    if stop == "ssm":
        K.wait_all("sp", [mixS, ST, kT[0], hT_ctx]); return nc, dict(peak=K.peak)
    ring.drain()
    K.release(mP2)

    mixA = K.sb("mixA", [128, AC, T], BF16, "R")
    mP2b = K.mark()
    ring = Ring(4, 128, "slab")
    cosO, sinO = make_rope(TC, T, "O")
    qTs = [K.sb(f"qT{i}", [128, 1024], BF16) for i in range(2)]
    scr_kv = (K.sb("kraw", [128, 1024], F32), K.sb("kro", [128, 1024], F32), qTs[1])
    raw, ro, _ = scr_kv
    qsq = raw
    kTo1 = K.sb("kTo", [128, T], BF16); vtoko1 = K.sb("vtoko", [128, NT, 132], BF16)
    K.op("dve", lambda e: e.memset(vtoko1.t[:, :, 128:132], 1.0), w=(vtoko1,))
    kTo = [kTo1] * NKV; vtoko = [vtoko1] * NKV
    kmeanT = K.sb("kmeanT", [128, 8], BF16)
    kmx = K.sb("kmx", [128, 1], F32)
    gb8 = K.sb("gb8", [128, 8, 8], F32)
    for qt in range(8):
        K.op("dve", lambda e: e.tensor_copy(gb8.t[:, qt, :], gbias.t[:, qt // 2, :]), r=(gbias,), w=(gb8,))
    gm = K.sb("gm", [128, 8, 8], F32); g2 = K.sb("g2", [128, 8, 8], F32); msk = K.sb("msk", [128, 8, 8], F32)
    m1 = K.sb("m1", [128, 8], F32)
    sels = [K.sb(f"sel{i}", [128, 8, 8], F32) for i in range(2)]
    cbs_ = [K.sb(f"cb{i}", [128, 1], F32) for i in range(2)]
    qmx = K.sb("qmx", [128, 2], F32)
    NPT = 4
    pT = [K.sb(f"pT{i}", [128, 256], BF16) for i in range(NPT)]
    Oacc = K.sb("Oacc", [128, 8, 132], F32)
    onrm = K.sb("onrm", [128, 128], BF16)
    rdc = K.sb("rdc", [128, 1], F32)
    RB = (0, 1)

    def ksl(kv, kt):
        return (kT[kv], kT[kv].t[:, kt * 128:(kt + 1) * 128]) if kt < 8 else (kTo[kv], kTo[kv].t[:, (kt - 8) * 128:(kt - 7) * 128])

    def vsl(kv, kt):
        return (vtok[kv], vtok[kv].t[:, kt, 0:129]) if kt < 8 else (vtoko[kv], vtoko[kv].t[:, kt - 8, 0:129])

    def bmax(dst, src):
        K.op("dve", lambda e: e.tensor_reduce(out=dst.t[:], in_=src.t[:], axis=AX.X, op=ALU.max), r=(src,), w=(dst,))

    def knock(dst, src, m):
        K.op("dve", lambda e: e.tensor_tensor(out=msk.t[:], in0=src.t[:], in1=m.t[:].unsqueeze(2).broadcast_to([128, 8, 8]),
                                              op=ALU.is_ge), r=(src, m), w=(msk,))
        K.op("dve", lambda e: e.scalar_tensor_tensor(out=dst.t[:], in0=msk.t[:], scalar=-3e30, in1=src.t[:],
                                                     op0=ALU.mult, op1=ALU.add), r=(msk, src), w=(dst,))

    def prepA(kv, g, qT_, cb_):
        def cbq(ct, bs):
            rope_head(bs, cosO, sinO, raw, ro, RB)
            copy("act", qT_.t[:], ro.t[:], r=(ro,), w=(qT_,))
            K.op("act", lambda e: e.activation(out=qsq.t[:], in_=ro.t[:], func=AF.Square), r=(ro,), w=(qsq,))
            for hh in range(2):
                K.op("pe", lambda e: e.matmul(pf(RB[hh])[:, :], lhsT=onesf.t[:], rhs=qsq.t[:, hh * 512:(hh + 1) * 512],
                                              start=True, stop=True), r=(onesf, qsq), w=(pb[RB[hh]],))
                K.op("dve", lambda e: e.tensor_reduce(out=qmx.t[:, hh:hh + 1], in_=pf(RB[hh])[:, :], axis=AX.X, op=ALU.max),
                     r=(pb[RB[hh]],), w=(qmx,))
            K.op("dve", lambda e: e.tensor_tensor(out=qmx.t[:, 0:1], in0=qmx.t[:, 0:1], in1=qmx.t[:, 1:2], op=ALU.max),
                 r=(qmx,), w=(qmx,))
            K.op("act", lambda e: e.activation(out=cb_.t[:], in_=qmx.t[:, 0:1], func=AF.Sqrt, scale=kmx.t[:, 0:1]),
                 r=(qmx, kmx), w=(cb_,))
            K.op("dve", lambda e: e.tensor_scalar(out=cb_.t[:], in0=cb_.t[:], scalar1=-SCALE, scalar2=None, op0=ALU.mult),
                 r=(cb_,), w=(cb_,))
        gemm_B(ring, w_in, q0 + (kv * 4 + g) * 128, 128, hT_own, 0, 1024, [0, 1], cbq)

    def prepB(qT_, sel_):
        for qt in range(NT):
            ts_ = slice(qt * 128, (qt + 1) * 128)
            K.op("pe", lambda e: e.matmul(pf(RB[0])[:, qt * 8:qt * 8 + 8], lhsT=qT_.t[:, ts_], rhs=kmeanT.t[:, 0:8], start=True, stop=True),
                 r=(qT_, kmeanT), w=(pb[RB[0]],), inc=(qt == NT - 1), acc=True)
        Gv = pf(RB[0])[:, 0:64].rearrange("p (t c) -> p t c", c=8)
        K.op("dve", lambda e: e.tensor_tensor(out=gm.t[:], in0=Gv, in1=gb8.t[:], op=ALU.add), r=(pb[RB[0]], gb8), w=(gm,))
        bmax(m1, gm); knock(g2, gm, m1)
        bmax(m1, g2); knock(g2, g2, m1)
        bmax(m1, g2)
        K.op("dve", lambda e: e.tensor_scalar(out=m1.t[:], in0=m1.t[:], scalar1=-1e29, scalar2=None, op0=ALU.max), r=(m1,), w=(m1,))
        K.op("dve", lambda e: e.tensor_tensor(out=sel_.t[:], in0=gm.t[:], in1=m1.t[:].unsqueeze(2).broadcast_to([128, 8, 8]),
                                              op=ALU.is_ge), r=(gm, m1), w=(sel_,))

    state = {"n": 0, "blk": 0}

    def main_items(kv, g):
        items = []
        for i in range(4):
            nbk = 4 + i + 1
            for j in range(nbk):
                for u in range(2):
                    own = (j == nbk - 1)
                    items.append(dict(i=i, j=j, kt=j * 2 + u, u=u, own=own))
        return items

    def emit_qk(kv, qT_, cb_, it):
        n = state["n"]; state["n"] += 1
        it["p"] = pT[n % NPT]
        bank = 2 + (n % 2)
        qs = slice(it["i"] * 256, (it["i"] + 1) * 256)
        kb, kap = ksl(kv, it["kt"])
        K.op("pe", lambda e: e.matmul(pf(bank)[:, 0:256], lhsT=kap, rhs=qT_.t[:, qs], start=True, stop=(not it["own"])),
             r=(kb, qT_), w=(pb[bank],), inc=(not it["own"]), acc=True)
        if it["own"]:
            K.op("pe", lambda e: e.matmul(pf(bank)[:, 0:256], lhsT=identb.t[:], rhs=maskT.t[:, it["u"], :], start=False, stop=True),
                 r=(identb, maskT), w=(pb[bank],), acc=True)
        K.op("act", lambda e: e.activation(out=it["p"].t[:], in_=pf(bank)[:, 0:256], func=AF.Exp, scale=SCALE, bias=cb_.t[:, 0:1]),
             r=(pb[bank], cb_), w=(it["p"],))

    def emit_pv(kv, g, sel_, it):
        par = state["blk"] % 2
        vb, vap = vsl(kv, it["kt"])
        for w_ in range(2):
            bank = 4 + par * 2 + w_
            K.op("pe", lambda e: e.matmul(pf(bank)[:, 0:129], lhsT=it["p"].t[:, w_ * 128:(w_ + 1) * 128], rhs=vap,
                                          start=(it["u"] == 0), stop=(it["u"] == 1)),
                 r=(it["p"], vb), w=(pb[bank],), inc=(it["u"] == 1), acc=True)
        if it["u"] == 1:
            for w_ in range(2):
                bank = 4 + par * 2 + w_
                qt = it["i"] * 2 + w_
                if it["j"] == 0:
                    K.op("dve", lambda e: e.tensor_scalar(out=Oacc.t[:, qt, 0:129], in0=pf(bank)[:, 0:129],
                                                          scalar1=sel_.t[:, qt, 0:1], scalar2=None, op0=ALU.mult),
                         r=(pb[bank], sel_), w=(Oacc,))
                elif not it["own"]:
                    K.op("dve", lambda e: e.scalar_tensor_tensor(out=Oacc.t[:, qt, 0:129], in0=pf(bank)[:, 0:129],
                                                                 scalar=sel_.t[:, qt, it["j"]:it["j"] + 1], in1=Oacc.t[:, qt, 0:129],
                                                                 op0=ALU.mult, op1=ALU.add), r=(pb[bank], sel_, Oacc), w=(Oacc,))
                else:
                    K.op("dve", lambda e: e.tensor_tensor(out=Oacc.t[:, qt, 0:129], in0=pf(bank)[:, 0:129], in1=Oacc.t[:, qt, 0:129],
                                                          op=ALU.add), r=(pb[bank], Oacc), w=(Oacc,))
                    K.op("dve", lambda e: e.reciprocal(out=rdc.t[:], in_=Oacc.t[:, qt, 128:129]), r=(Oacc,), w=(rdc,))
                    K.op("dve", lambda e: e.tensor_scalar(out=onrm.t[:], in0=Oacc.t[:, qt, 0:128], scalar1=rdc.t[:, 0:1], scalar2=None,
                                                          op0=ALU.mult), r=(Oacc, rdc), w=(onrm,))
                    K.op("pe", lambda e: e.transpose(out=pbf(RB[1])[:, 0:128], in_=onrm.t[:], identity=identb.t[:]),
                         r=(onrm, identb), w=(pb[RB[1]],))
                    copy("act", mixA.t[:, kv * 4 + g, qt * 128:(qt + 1) * 128], pbf(RB[1])[:, 0:128], r=(pb[RB[1]],), w=(mixA,))
            state["blk"] += 1

    LA = 3
    for kv in range(NKV):
        kv_phase(hT_own, 4, cosO, sinO, ring, scr_kv, [kv], kTo, vtoko, RB)
        K.op("dve", lambda e: e.tensor_scalar(out=kmeanT.t[:], in0=kms.t[:, kv, :], scalar1=1.0 / 256, scalar2=None, op0=ALU.mult),
             r=(kms,), w=(kmeanT,))
        K.op("dve", lambda e: e.tensor_reduce(out=kmx.t[:], in_=kmax.t[:, kv, :], axis=AX.X, op=ALU.max), r=(kmax,), w=(kmx,))
        prepA(kv, 0, qTs[0], cbs_[0]); prepB(qTs[0], sels[0])
        for g in range(4):
            qT_, sel_, cb_ = qTs[g % 2], sels[g % 2], cbs_[g % 2]
            items = main_items(kv, g)
            ni = len(items)
            hooks = {}
            if g < 3:
                nq, nsl, ncb = qTs[(g + 1) % 2], sels[(g + 1) % 2], cbs_[(g + 1) % 2]
                hooks = {ni // 8: (lambda nq=nq, ncb=ncb, g=g: prepA(kv, g + 1, nq, ncb)),
                         (5 * ni) // 8: (lambda nq=nq, nsl=nsl: prepB(nq, nsl))}
            for n in range(ni + LA):
                if n in hooks:
                    hooks[n]()
                if n < ni:
                    emit_qk(kv, qT_, cb_, items[n])
                if n - LA >= 0:
                    emit_pv(kv, g, sel_, items[n - LA])
    dump("attT", mixA)
    if stop == "att":
        K.wait_all("sp", [mixA, mixS, ST, kT[0], hT_ctx]); return nc, dict(peak=K.peak)
    ring.drain()
    K.release(mP2b)

    K.release((mL0[0], K.hi, mL0[2]))
    x1buf = Buf(None, "x1_dram")
    mP3 = K.mark()
    ringA = Ring(3, 512, "slabA")
    wpost = K.sb("wpost", [128, D], F32)
    K.dma("sp", wpost.t[:], wpost1_d, w=(wpost,), chan=wpost)
    mixed = K.sb("mixed", [128, 4, D], F32)
    xt = [K.sb("xt3", [128, D], F32)]
    st3 = K.sb("st3", [128, 2], F32)

    def norm_res_store(src, tt, gt, x_src_d, x_rbuf, dst_d, dst_buf, wp, x_, wp_d=None):
        K.op("act", lambda e: e.activation(out=x_.t[:], in_=src.t[:, tt, :], func=AF.Square, accum_out=st3.t[:, 0:1]),
             r=(src,), w=(x_, st3))
        K.op("act", lambda e: e.activation(out=st3.t[:, 1:2], in_=st3.t[:, 0:1], func=AF.Sqrt, bias=epsc.t[:, 0:1], scale=1.0 / D),
             r=(st3, epsc), w=(st3,))
        K.op("dve", lambda e: e.reciprocal(out=st3.t[:, 1:2], in_=st3.t[:, 1:2]), r=(st3,), w=(st3,))
        K.dma("sp", x_.t[:], x_src_d[gt * 128:(gt + 1) * 128, :], r=x_rbuf, w=(x_,), chan=x_)
        if wp_d is None:
            K.op("dve", lambda e: e.scalar_tensor_tensor(out=src.t[:, tt, :], in0=src.t[:, tt, :], scalar=st3.t[:, 1:2], in1=wp.t[:],
                                                         op0=ALU.mult, op1=ALU.mult), r=(src, st3, wp), w=(src,))
        else:
            hd = D // 2
            for hf in range(2):
                K.dma("sp", wp.t[:], wp_d[:, hf * hd:(hf + 1) * hd], w=(wp,), chan=wp)
                K.op("dve", lambda e: e.scalar_tensor_tensor(out=src.t[:, tt, hf * hd:(hf + 1) * hd], in0=src.t[:, tt, hf * hd:(hf + 1) * hd],
                                                             scalar=st3.t[:, 1:2], in1=wp.t[:], op0=ALU.mult, op1=ALU.mult),
                     r=(src, st3, wp), w=(src,))
        K.op("dve", lambda e: e.tensor_tensor(out=x_.t[:], in0=x_.t[:], in1=src.t[:, tt, :], op=ALU.add), r=(x_, src), w=(x_,))
        K.dma("sp", dst_d[gt * 128:(gt + 1) * 128, :], x_.t[:], r=(x_,), w=(dst_buf,), chan=x_)

    def gemm_A(ring_, w_d, nK, actbuf, tok_tiles, dst, ncg):
        ntt = len(tok_tiles)
        for cg in range(ncg):
            base = (cg % 2) * 4
            for s0 in range(0, nK, 8):
                nk = min(8, nK - s0)
                sl = ring_.load(w_d, s0 * 128, nk, cg * 512, 512)
                for j in range(nk):
                    kc = s0 + j
                    for i, t0 in enumerate(tok_tiles):
                        last = (kc == nK - 1)
                        ab, akc = actbuf(kc) if callable(actbuf) else (actbuf, kc)
                        K.op("pe", lambda e: e.matmul(pf(base + i)[:, :], lhsT=ab.t[:, akc, t0:t0 + 128], rhs=sl.t[:, j, :],
                                                      start=(kc == 0), stop=last), r=(ab, sl), w=(pb[base + i],),
                             inc=(last or (j == nk - 1 and i == ntt - 1)), acc=True)
            for i in range(ntt):
                copy(ev_eng(), dst.t[:, i, cg * 512:(cg + 1) * 512], pf(base + i)[:, :], r=(pb[base + i],), w=(dst,))

    mixmap = lambda kc: (mixA, kc) if kc < AC else (mixS, kc - AC)
    for tg in range(2):
        gemm_A(ringA, w_out, MC, mixmap, [(tg * 4 + i) * 128 for i in range(4)], mixed, D // 512)
        for tt in range(4):
            gt = tg * 4 + tt
            norm_res_store(mixed, tt, gt, x_own, (), x1_d, x1buf, wpost, xt[0])
    ringA.drain()
    K.release(mP3)
    K.release((K.lo, mR[1], mR[2]))
    if stop == "mix":
        K.wait_all("sp", [x1buf]); return nc, dict(peak=K.peak)

    outbuf = Buf(None, "out_dram")
    K.release((mL0[0], K.nc_total, 0))
    identb = K.sb("identb2", [128, 128], BF16)
    K.dma("sp", identb.t[:], identb_d, w=(identb,), chan=identb)
    wpre2 = K.sb("wpre2b", [128, KC], F32)
    K.dma("sp", wpre2.t[:], wpre2_d, w=(wpre2,), chan=wpre2)
    epsc = K.sb("epsc2", [128, 1], F32)
    K.op("dve", lambda e: e.memset(epsc.t[:], EPS), w=(epsc,))
    for tg in range(2):
        mF = K.mark()
        actT = K.sb("actT", [128, FC, 512], BF16)
        mF2 = K.mark()
        h2T = K.sb("h2T", [128, KC, 512], BF16)
        build_hT(x1_d[tg * 512:(tg + 1) * 512, :], 512, h2T, wpre2b_d, rbuf=(x1buf,))
        ring = Ring(8, 256, "slabF")
        sg_ = K.sb("sgate", [128, 512], F32)
        for gi, cc0 in enumerate(range(0, DFF, 256)):
            ncols = min(256, DFF - cc0)
            nct = (ncols + 127) // 128
            bset = (gi % 2) * 4
            gemm_B(ring, w_gate, cc0, ncols, h2T, 0, 512, [bset, bset + 1][:nct], lambda ct, bs: None)

            def cbu(ct, bs, cc0=cc0, bset=bset):
                jt = cc0 // 128 + ct
                K.op("act", lambda e: e.activation(out=sg_.t[:], in_=pf(bset + ct)[:, :], func=AF.Silu), r=(pb[bset + ct],), w=(sg_,))
                K.op("dve", lambda e: e.tensor_tensor(out=actT.t[:, jt, :], in0=sg_.t[:], in1=pf(bs[0])[:, :], op=ALU.mult),
                     r=(sg_, pb[bs[0]]), w=(actT,))
            gemm_B(ring, w_up, cc0, ncols, h2T, 0, 512, [bset + 2, bset + 3][:nct], cbu)
        ring.drain()
        K.release(mF2)
        ringA = Ring(3, 512, "slabD")
        wph = K.sb("wpost2h", [128, D // 2], F32)
        f = K.sb("f", [128, 4, D], F32)
        xt = [K.sb("xt4", [128, D], F32)]
        st3 = K.sb("st4", [128, 2], F32)
        gemm_A(ringA, w_down, FC, actT, [i * 128 for i in range(4)], f, D // 512)
        for tt in range(4):
            gt = tg * 4 + tt
            norm_res_store(f, tt, gt, x1_d, (x1buf,), out_d, outbuf, wph, xt[0], wp_d=wpost2_d)
        ringA.drain()
        K.release(mF)
    K.wait_all("sp", [outbuf])
    K.flush()
    return nc, dict(peak=K.peak)

    outs = [b for b in [ST, kT[0], hT_ctx] if b.chan is not None]
    K.wait_all("sp", outs)
    K.flush()
    return nc, dict(peak=K.peak)


def host_consts(cfg):
    D = cfg["D"]; G = cfg["G"]
    c = {}
    c["identb"] = np.eye(128, dtype=np.float32).astype(ml_dtypes.bfloat16)
    c["identf"] = np.eye(128, dtype=np.float32)
    pm = np.zeros((128, 128), np.float32)
    for d in range(64):
        pm[d + 64, d] = -1.0
        pm[d, d + 64] = 1.0
    c["pmT"] = pm
    tri = np.zeros((128, 2, 256), np.float32)
    mT = np.zeros((128, 2, 256), np.float32)
    for j in range(2):
        s = j * 128 + np.arange(128)[:, None]
        l = np.arange(256)[None, :]
        tri[:, j, :] = (s <= l)
        mT[:, j, :] = np.where(l >= s, 0.0, NEG)
    c["tri"] = tri.reshape(128, 512)
    c["maskT"] = mT.reshape(128, 512).astype(ml_dtypes.bfloat16)
    t = np.arange(128)[:, None]; q = np.arange(128)[None, :]
    c["trib"] = np.where(t <= q, 0.0, NEG).astype(np.float32).astype(ml_dtypes.bfloat16)
    sel8 = np.zeros((128, 8, 128), np.float32)
    for h in range(8):
        sel8[h, h, :] = 1.0
    c["sel8"] = sel8.reshape(128, 1024)
    oh9 = np.zeros((128, 9, 128), np.float32)
    for h in range(9):
        oh9[h, h, :] = 1.0
    c["oh9"] = oh9.reshape(128, 9 * 128).astype(ml_dtypes.bfloat16)
    half = 64
    inv = (10000.0 ** (-np.arange(half, dtype=np.float32) / half)).astype(np.float32)
    c["invf"] = np.concatenate([inv, inv]).reshape(128, 1).astype(np.float32)
    return c


def host_inputs(cfg, inputs, core):
    D = cfg["D"]; G = cfg["G"]; NKV = cfg["NKV"]
    NH = G * 8
    b, half = core // 2, core % 2
    m = {}
    x = inputs["x"]
    m["x_own"] = np.ascontiguousarray(x[b, half * T:(half + 1) * T])
    m["x_ctx"] = np.ascontiguousarray(x[b, 0:TC]) if half == 1 else np.zeros((TC, D), np.float32)
    p = inputs["positions"][b].astype(np.int32)
    if half == 1:
        pp = np.concatenate([p[0:TC], p[TC:TC + T]])
    else:
        pp = np.concatenate([np.zeros(TC, np.int32), p[0:T]])
    m["pos"] = np.ascontiguousarray(np.broadcast_to(pp[None, :], (128, TC + T)))
    m["flag"] = np.full((128, 1), float(half), np.float32)
    gb = np.zeros((4, 8), np.float32)
    for i in range(4):
        for j in range(8):
            valid = (j < 4 + i) and (j >= 4 or half == 1)
            gb[i, j] = 0.0 if valid else -1e30
    m["gbias"] = np.ascontiguousarray(np.broadcast_to(gb.reshape(1, 32), (128, 32)))
    m["w_in"] = inputs["w_in"][0]; m["w_out"] = inputs["w_out"][0]
    m["w_gate"] = inputs["w_gate"][0]; m["w_up"] = inputs["w_up"][0]; m["w_down"] = inputs["w_down"][0]
    col = lambda v: np.ascontiguousarray(v.reshape(-1, 128).T)
    rep = lambda v: np.ascontiguousarray(np.broadcast_to(v.reshape(1, -1), (128, v.size)))
    m["wpre1"] = col(inputs["mix_pre_norm"][0]); m["wpre2"] = col(inputs["ffn_pre_norm"][0])
    m["wpost1"] = rep(inputs["mix_post_norm"][0]); m["wpost2"] = rep(inputs["ffn_post_norm"][0])
    m["wpre1b"] = rep(inputs["mix_pre_norm"][0]); m["wpre2b"] = rep(inputs["ffn_pre_norm"][0])
    cw = inputs["conv_w"][0]
    NCT = cw.shape[1] // 128
    m["convw"] = np.ascontiguousarray(cw.T.reshape(NCT, 128, 4).transpose(1, 0, 2).reshape(128, NCT * 4))
    m["convb"] = col(inputs["conv_b"][0])
    m["dtb"] = rep(inputs["dt_bias"][0]); m["alog"] = rep(inputs["a_log"][0])
    m["dcol"] = col(np.repeat(inputs["d_skip"][0], 64)); m["ncol"] = col(inputs["ssm_norm"][0])
    m.update(host_consts(cfg))
    return m


def kernel(**inputs):
    cfg = FULL_CFG
    inputs = {k: np.asarray(v) for k, v in inputs.items()}
    nc, _ = build(cfg)
    n = 2 * cfg["B"]
    in_maps = [host_inputs(cfg, inputs, c) for c in range(n)]
    res = run_bass_kernel_spmd(nc, in_maps, core_ids=list(range(n)))
    out = np.empty((cfg["B"], 2 * T, cfg["D"]), np.float32)
    for c in range(n):
        out[c // 2, (c % 2) * T:(c % 2 + 1) * T] = np.asarray(res.results[c]["out"], dtype=np.float32)
    return out
```
